# Optimizing a Trainium2 kernel written in Bass

```python
import math
import jax
import jax.numpy as jnp
from jax import lax
import numpy as np

D_MODEL = 1024
BATCH = 32
SEQ = 256
DEPTH = 4
DEC_BATCH = 4
DEC_SEQ = 4096
PAST_LEN = 512

GRID_W = 64
N_HEADS_A = 4
HEAD_DK = 128
HEAD_DV = 128
D_A = N_HEADS_A * HEAD_DV
D_B = 512
D_MIX = D_A + D_B
SHORT_CONV = 5
CF_CONV = 31
FFN_CONV = 3
D_FF = 2816
CHUNK = 64
N_MOD = 6
IN_COLS = 4 * D_A + 4 * N_HEADS_A + 2 * D_B
EPS = 1e-6

kernel_name = "hybrid_deltanet_conformer_dit_step"


def rmsnorm(x, g):
    xf = x.astype(jnp.float32)
    y = xf * lax.rsqrt(jnp.mean(xf * xf, axis=-1, keepdims=True) + EPS)
    return (y * g.astype(jnp.float32)).astype(x.dtype)


def layernorm(x, g, b):
    xf = x.astype(jnp.float32)
    mu = jnp.mean(xf, axis=-1, keepdims=True)
    var = jnp.mean(jnp.square(xf - mu), axis=-1, keepdims=True)
    y = (xf - mu) * lax.rsqrt(var + EPS)
    return (y * g.astype(jnp.float32) + b.astype(jnp.float32)).astype(x.dtype)


def l2norm(x):
    return x * lax.rsqrt(jnp.sum(x * x, axis=-1, keepdims=True) + EPS)


def dwconv1d(x, w):
    c_dim = x.shape[-1]
    return lax.conv_general_dilated(
        x, w[:, None, :].astype(x.dtype), window_strides=(1,), padding="SAME",
        dimension_numbers=("NWC", "WIO", "NWC"), feature_group_count=c_dim)


def dwconv2d_grid(x, w):
    b_dim, t_dim, c_dim = x.shape
    rows = t_dim // GRID_W
    xg = x.reshape(b_dim, rows, GRID_W, c_dim)
    y = lax.conv_general_dilated(
        xg, w[:, :, None, :].astype(x.dtype), window_strides=(1, 1), padding="SAME",
        dimension_numbers=("NHWC", "HWIO", "NHWC"), feature_group_count=c_dim)
    return y.reshape(b_dim, t_dim, c_dim)


def gated_delta_chunked(q, k, v, g, beta, s0):
    f32 = jnp.float32
    b_dim, t_dim, h_dim, dk = q.shape
    dv = v.shape[-1]
    n = t_dim // CHUNK

    def to_chunks(t):
        t = t.astype(f32).reshape((b_dim, n, CHUNK, h_dim) + t.shape[3:])
        return jnp.moveaxis(t, 3, 1)

    qc = to_chunks(l2norm(q.astype(f32)) * (dk ** -0.5))
    kc = to_chunks(l2norm(k.astype(f32)))
    vc = to_chunks(v)
    gc = jnp.cumsum(to_chunks(g), axis=-1)
    bc = to_chunks(beta)

    idx = jnp.arange(CHUNK)
    incl = idx[:, None] >= idx[None, :]
    strict = idx[:, None] > idx[None, :]
    diff = gc[..., :, None] - gc[..., None, :]
    decay = jnp.where(incl, jnp.exp(jnp.where(incl, diff, 0.0)), 0.0)
    kb = kc * bc[..., None]
    lmat = jnp.where(strict, jnp.einsum("bhncd,bhnsd->bhncs", kb, kc) * decay, 0.0)
    amat = lmat + jnp.eye(CHUNK, dtype=f32)
    rhs = jnp.concatenate([vc * bc[..., None], kb * jnp.exp(gc)[..., None]], axis=-1)
    sol = lax.linalg.triangular_solve(amat, rhs, left_side=True, lower=True,
                                      unit_diagonal=True)
    uc, wc = sol[..., :dv], sol[..., dv:]
    qk = jnp.einsum("bhncd,bhnsd->bhncs", qc, kc) * decay

    def step(s, inp):
        q_i, k_i, u_i, w_i, g_i, qk_i = inp
        v_new = u_i - jnp.einsum("bhck,bhkv->bhcv", w_i, s)
        o_i = (jnp.einsum("bhck,bhkv->bhcv", q_i * jnp.exp(g_i)[..., None], s)
               + jnp.einsum("bhcs,bhsv->bhcv", qk_i, v_new))
        g_last = g_i[..., -1:]
        s = (s * jnp.exp(g_last)[..., None]
             + jnp.einsum("bhck,bhcv->bhkv", k_i * jnp.exp(g_last - g_i)[..., None], v_new))
        return s, o_i

    xs = tuple(jnp.moveaxis(t, 2, 0) for t in (qc, kc, uc, wc, gc, qk))
    s_fin, oc = lax.scan(step, s0.astype(f32), xs)
    o = jnp.transpose(oc, (1, 0, 3, 2, 4)).reshape(b_dim, t_dim, h_dim, dv)
    return o, s_fin


def trunk_layer(x, mod, s_init, grid, w_in, conv_qkv, a_log, dt_bias, g_onorm,
                conv_cf, b_conv_cf, ln_cf_g, ln_cf_b, w_out, w_up, conv_ffn, b_conv_ffn,
                w_down, g_pre_mix, g_post_mix, g_pre_ffn, g_post_ffn):
    b_dim, t_dim, _ = x.shape
    f32 = jnp.float32
    shift_m, scale_m, gate_m, shift_f, scale_f, gate_f = jnp.split(mod[:, None, :], N_MOD, axis=-1)

    h = rmsnorm(x, g_pre_mix) * (1.0 + scale_m) + shift_m
    proj = h @ w_in
    qkv, z, ab, glu = jnp.split(proj, [3 * D_A, 4 * D_A, 4 * D_A + 4 * N_HEADS_A], axis=-1)

    qkv = jax.nn.silu(dwconv1d(qkv, conv_qkv))
    q, k, v = jnp.split(qkv, 3, axis=-1)
    q = q.reshape(b_dim, t_dim, N_HEADS_A, HEAD_DK)
    k = k.reshape(b_dim, t_dim, N_HEADS_A, HEAD_DK)
    v = v.reshape(b_dim, t_dim, N_HEADS_A, HEAD_DV)
    a_raw = ab[..., :2 * N_HEADS_A].reshape(b_dim, t_dim, 2, N_HEADS_A).astype(f32)
    b_raw = ab[..., 2 * N_HEADS_A:].reshape(b_dim, t_dim, 2, N_HEADS_A).astype(f32)
    g = -jnp.exp(a_log.astype(f32)) * jax.nn.softplus(a_raw + dt_bias.astype(f32))
    beta = jax.nn.sigmoid(b_raw)
    o_fwd, s_fwd = gated_delta_chunked(q, k, v, g[:, :, 0], beta[:, :, 0], s_init[:, 0])
    o_bwd, s_bwd = gated_delta_chunked(q[:, ::-1], k[:, ::-1], v[:, ::-1],
                                       g[:, ::-1, 1], beta[:, ::-1, 1], s_init[:, 1])
    o = o_fwd + o_bwd[:, ::-1]
    o = o * lax.rsqrt(jnp.mean(o * o, axis=-1, keepdims=True) + EPS) * g_onorm.astype(f32)
    o = o * jax.nn.silu(z.reshape(b_dim, t_dim, N_HEADS_A, HEAD_DV).astype(f32))
    o = o.reshape(b_dim, t_dim, D_A).astype(x.dtype)

    glu_a, glu_b = jnp.split(glu, 2, axis=-1)
    u = glu_a * jax.nn.sigmoid(glu_b)
    u = dwconv1d(u, conv_cf) + b_conv_cf
    u = jax.nn.silu(layernorm(u, ln_cf_g, ln_cf_b))

    mix = jnp.concatenate([o, u], axis=-1) @ w_out
    x = x + gate_m * rmsnorm(mix, g_post_mix)

    h = rmsnorm(x, g_pre_ffn) * (1.0 + scale_f) + shift_f
    gt, up = jnp.split(h @ w_up, 2, axis=-1)
    gt = (dwconv2d_grid(gt, conv_ffn) if grid else dwconv1d(gt, conv_ffn[1])) + b_conv_ffn
    y = (jax.nn.silu(gt) * up) @ w_down
    x = x + gate_f * rmsnorm(y, g_post_ffn)
    return x, jnp.stack([s_fwd, s_bwd], axis=1)


def setup_inputs(seed: int = 0) -> dict:
    key = jax.random.key(seed)
    ks = jax.random.split(key, 32)
    f32 = jnp.float32
    nrm = lambda k_, shape, s: jax.random.normal(k_, shape, f32) * s
    gain = lambda k_, shape: 1.0 + nrm(k_, shape, 0.02)
    return {
        "x_prompt": nrm(ks[0], (BATCH, SEQ, D_MODEL), 1.0),
        "x_sample": nrm(ks[1], (DEC_BATCH, DEC_SEQ, D_MODEL), 1.0),
        "state_delta": nrm(ks[2], (DEC_BATCH, DEPTH, 2, N_HEADS_A, HEAD_DK, HEAD_DV), 0.5),
        "c": nrm(ks[3], (DEC_BATCH, D_MODEL), 1.0),
        "c_ctx": nrm(ks[4], (D_MODEL,), 1.0),
        "w_mod": nrm(ks[5], (DEPTH, D_MODEL, N_MOD * D_MODEL), 0.5 * D_MODEL ** -0.5),
        "b_mod": nrm(ks[6], (DEPTH, N_MOD * D_MODEL), 0.02),
        "g_pre_mix": gain(ks[7], (DEPTH, D_MODEL)),
        "g_post_mix": gain(ks[8], (DEPTH, D_MODEL)),
        "g_pre_ffn": gain(ks[9], (DEPTH, D_MODEL)),
        "g_post_ffn": gain(ks[10], (DEPTH, D_MODEL)),
        "w_in": nrm(ks[11], (DEPTH, D_MODEL, IN_COLS), D_MODEL ** -0.5),
        "conv_qkv": nrm(ks[12], (DEPTH, SHORT_CONV, 3 * D_A), SHORT_CONV ** -0.5),
        "a_log": jnp.log(jax.random.uniform(ks[13], (DEPTH, 2, N_HEADS_A), f32, 1.0, 16.0)),
        "dt_bias": -3.0 + nrm(ks[14], (DEPTH, 2, N_HEADS_A), 0.5),
        "g_onorm": gain(ks[15], (DEPTH, HEAD_DV)),
        "conv_cf": nrm(ks[16], (DEPTH, CF_CONV, D_B), CF_CONV ** -0.5),
        "b_conv_cf": nrm(ks[17], (DEPTH, D_B), 0.02),
        "ln_cf_g": gain(ks[18], (DEPTH, D_B)),
        "ln_cf_b": nrm(ks[19], (DEPTH, D_B), 0.02),
        "w_out": nrm(ks[20], (DEPTH, D_MIX, D_MODEL), D_MIX ** -0.5),
        "w_up": nrm(ks[21], (DEPTH, D_MODEL, 2 * D_FF), D_MODEL ** -0.5),
        "conv_ffn": nrm(ks[22], (DEPTH, FFN_CONV, FFN_CONV, D_FF), 1.0 / FFN_CONV),
        "b_conv_ffn": nrm(ks[23], (DEPTH, D_FF), 0.02),
        "w_down": nrm(ks[24], (DEPTH, D_FF, D_MODEL), D_FF ** -0.5),
    }


def reference(x_prompt, x_sample, state_delta, c, c_ctx, w_mod, b_mod, g_pre_mix, g_post_mix,
              g_pre_ffn, g_post_ffn, w_in, conv_qkv, a_log, dt_bias, g_onorm, conv_cf,
              b_conv_cf, ln_cf_g, ln_cf_b, w_out, w_up, conv_ffn, b_conv_ffn, w_down):
    xp = x_prompt
    xs = x_sample
    s_zero = jnp.zeros((xp.shape[0], 2, N_HEADS_A, HEAD_DK, HEAD_DV), jnp.float32)
    ctx_states = []
    for l in range(DEPTH):
        mod_ctx = (jax.nn.silu(c_ctx) @ w_mod[l] + b_mod[l])[None, :]
        mod_lat = jax.nn.silu(c) @ w_mod[l] + b_mod[l]
        weights = (w_in[l], conv_qkv[l], a_log[l], dt_bias[l], g_onorm[l], conv_cf[l],
                   b_conv_cf[l], ln_cf_g[l], ln_cf_b[l], w_out[l], w_up[l], conv_ffn[l],
                   b_conv_ffn[l], w_down[l], g_pre_mix[l], g_post_mix[l], g_pre_ffn[l],
                   g_post_ffn[l])
        xp, s_ctx = trunk_layer(xp, mod_ctx, s_zero, False, *weights)
        ctx_states.append(s_ctx)
        xs, _ = trunk_layer(xs, mod_lat, state_delta[:, l], True, *weights)
    new_state_delta = jnp.stack(ctx_states, axis=1)
    return (xp, xs, new_state_delta)
```

```python
import numpy as np
from contextlib import ExitStack
import concourse.bass as bass
import concourse.mybir as mybir
from concourse.bass_utils import run_bass_kernel_spmd

F32 = mybir.dt.float32
BF16 = mybir.dt.bfloat16
AF = mybir.ActivationFunctionType
ALU = mybir.AluOpType
AX = mybir.AxisListType

ENGS = ("pe", "act", "dve", "pool", "sp")
PSUM_KEYS = set("ps%d" % i for i in range(8))


class Prog:
    EPOCH = 20000
    NDMA = 12

    def __init__(self, nc, es):
        self.nc = nc
        self.es = es
        self.q = {e: [] for e in ENGS}
        self.cnt = {e: 0 for e in ENGS}
        self.seen = {e: {} for e in ENGS}
        self.track = {}
        self.dma_val = {}
        self.dma_rr = {e: 0 for e in ENGS}
        self.semkeys = set()
        self.nbuf = 0

    def sb(self, name, shape, dt):
        return self.es.enter_context(self.nc.sbuf_tensor(name, list(shape), dt))

    def ps(self, name, shape, dt):
        return self.es.enter_context(self.nc.psum_tensor(name, list(shape), dt))

    def _need(self, eng, tok, waits):
        if tok is None:
            return
        key, val = tok
        if key[0] == "eng" and key[1] == "pe" and eng == "pe":
            return
        s = self.seen[eng]
        if key[0] == "eng":
            for (k2, v2) in s.items():
                if k2[0] == "eng" and k2[1] == key[1] and k2[2] > key[2]:
                    return
        if s.get(key, 0) >= val:
            return
        waits[key] = max(waits.get(key, 0), val)

    def _deps(self, eng, reads, writes):
        waits = {}
        for k in reads:
            t = self.track.get(k)
            if t:
                self._need(eng, t["w"], waits)
        for k in writes:
            t = self.track.get(k)
            if t:
                self._need(eng, t["w"], waits)
                for tok in t["r"].values():
                    self._need(eng, tok, waits)
        for key, val in waits.items():
            self.seen[eng][key] = val
        return list(waits.items())

    def _commit(self, tok, reads, writes):
        for k in reads:
            t = self.track.setdefault(k, {"w": None, "r": {}})
            t["r"][tok[0]] = tok
        for k in writes:
            self.track[k] = {"w": tok, "r": {}}

    def op(self, eng, fn, reads=(), writes=()):
        writes = list(writes) + [k for k in reads if k in PSUM_KEYS]
        reads = [k for k in reads if k not in PSUM_KEYS]
        waits = self._deps(eng, reads, writes)
        self.cnt[eng] += 1
        n = self.cnt[eng]
        key = ("eng", eng, (n - 1) // self.EPOCH)
        tok = (key, (n - 1) % self.EPOCH + 1)
        self.semkeys.add(key)
        for k, _ in waits:
            self.semkeys.add(k)
        self.q[eng].append((waits, fn, key, 1))
        self._commit(tok, reads, writes)
        return tok

    def dma(self, queue, out_ap, in_ap, reads=(), writes=(), **kw):
        j = self.dma_rr[queue]
        self.dma_rr[queue] = (j + 1) % self.NDMA
        key = ("dma", queue, j)
        prev = self.dma_val.get(key, 0)
        waits = dict(self._deps(queue, reads, writes))
        if prev and self.seen[queue].get(key, 0) < prev:
            waits[key] = prev
            self.seen[queue][key] = prev
        val = prev + 16
        self.dma_val[key] = val
        self.semkeys.add(key)
        for k in waits:
            self.semkeys.add(k)
        fn = (lambda e, o=out_ap, i=in_ap, kw=kw: e.dma_start(out=o, in_=i, **kw))
        self.q[queue].append((list(waits.items()), fn, key, 16))
        tok = (key, val)
        self._commit(tok, reads, writes)
        return tok

    def wait_all(self, eng, keys):
        waits = self._deps(eng, keys, ())
        for k, _ in waits:
            self.semkeys.add(k)
        self.q[eng].append((waits, None, None, 0))

    def finish(self):
        nc = self.nc
        sems = {}
        for key in sorted(self.semkeys, key=str):
            sems[key] = self.es.enter_context(nc.semaphore("s_" + "_".join(str(x) for x in key)))
        handles = {"pe": "tensor", "act": "scalar", "dve": "vector", "pool": "gpsimd", "sp": "sync"}
        with nc.Block() as block:
            for eng in ENGS:
                lst = self.q[eng]
                if not lst:
                    continue

                def body(e, lst=lst):
                    for waits, fn, key, inc in lst:
                        for wk, wv in waits:
                            e.wait_ge(sems[wk], wv)
                        if fn is not None:
                            fn(e).then_inc(sems[key], inc)

                getattr(block, handles[eng])(body)

    def barrier(self):
        for eng in ENGS:
            waits = {}
            for other in ENGS:
                n = self.cnt[other]
                if other != eng and n > 0:
                    self._need(eng, (("eng", other, (n - 1) // self.EPOCH), (n - 1) % self.EPOCH + 1), waits)
            for key, val in self.dma_val.items():
                self._need(eng, (key, val), waits)
            for key, val in waits.items():
                self.seen[eng][key] = val
                self.semkeys.add(key)
            self.q[eng].append((list(waits.items()), None, None, 0))

    def copy(self, eng, out, in_, r, w):
        if eng == "act":
            return self.op("act", lambda e: e.copy(out, in_), r, w)
        return self.op(eng, lambda e: e.tensor_copy(out, in_), r, w)

    def tt(self, eng, out, a, b, op, r, w):
        return self.op(eng, lambda e: e.tensor_tensor(out, a, b, op), r, w)

    def ts(self, eng, out, a, s1, s2, op0, op1, r, w):
        if s2 is None:
            return self.op(eng, lambda e: e.tensor_scalar(out, a, s1, None, op0), r, w)
        return self.op(eng, lambda e: e.tensor_scalar(out, a, s1, s2, op0, op1), r, w)

    def stt(self, eng, out, a, s, b, op0, op1, r, w):
        return self.op(eng, lambda e: e.scalar_tensor_tensor(out, a, s, b, op0, op1), r, w)

    def actf(self, out, in_, func, r, w, bias=0.0, scale=1.0, accum=None):
        if accum is None:
            return self.op("act", lambda e: e.activation(out, in_, func, bias=bias, scale=scale), r, w)
        return self.op("act", lambda e: e.activation(out, in_, func, bias=bias, scale=scale, accum_out=accum), r, w)

    def mm(self, out, lhsT, rhs, start, stop, r, w):
        return self.op("pe", lambda e: e.matmul(out, lhsT, rhs, start=start, stop=stop), r, w)

    def tr(self, out, in_, ident, r, w):
        return self.op("pe", lambda e: e.transpose(out, in_, ident), r, w)

    def memset(self, eng, ap, val, w):
        return self.op(eng, lambda e: e.memset(ap, val), (), w)


def bc_last(ap, n):
    sh = list(ap.shape)
    return ap.unsqueeze(len(sh)).to_broadcast(sh + [n])


def bc_mid(ap, n):
    sh = list(ap.shape)
    return ap.unsqueeze(1).to_broadcast([sh[0], n] + sh[1:])


D = 1024
DA = 512
NH = 4
DFF = 2816
INC = 3088
SEG = 256
CH = 64
NMOD = 6
EPS = 1e-6
PADQ = 2
PADC = 15
PADF = 66
OB_BMOD, OB_GPM, OB_GQM, OB_GPF, OB_GQF, OB_CQ, OB_CCF, OB_BCF, OB_LNG, OB_LNB, OB_GON, OB_BFF, OB_CFFN = (
    0, 48, 56, 64, 72, 80, 140, 264, 268, 272, 276, 277, 299)
NV = 497


class Ctx:
    pass


def build_program(NT, DEPTH, debug=False, stop_after=None):
    assert NT % 512 == 0
    NSEG = NT // SEG
    NTT = NT // 512
    NB = NT // 128
    NCH = NT // CH
    nc = bass.Bass("TRN2", target_bir_lowering=False)

    def din(name, shape, dt=F32):
        return nc.dram_tensor(name, list(shape), dt, kind="ExternalInput").ap()

    def dout(name, shape, dt=F32):
        return nc.dram_tensor(name, list(shape), dt, kind="ExternalOutput").ap()

    def scr(name, shape, dt):
        if debug:
            return nc.dram_tensor(name, list(shape), dt, kind="ExternalOutput").ap()
        return nc.dram_tensor(name, list(shape), dt).ap()

    g = Ctx()
    g.NT, g.DEPTH, g.NSEG, g.NTT, g.NB, g.NCH = NT, DEPTH, NSEG, NTT, NB, NCH
    g.x_in = din("x", [NT, D])
    g.cvec = din("cvec", [128, 8])
    g.vecP = din("vecP", [DEPTH, 128, NV])
    g.vecB = din("vecB", [DEPTH, 16])
    g.s0 = din("s0", [DEPTH, 2, NH, 128, 128])
    g.keep_d = din("keep", [128, 1])
    g.cmask = din("cmask", [2, NT])
    g.consts_d = din("consts", [128, 512])
    g.w_mod = din("w_mod", [DEPTH, D, NMOD * D])
    g.w_in = din("w_in", [DEPTH, D, INC])
    g.w_out = din("w_out", [DEPTH, D, D])
    g.w_up = din("w_up", [DEPTH, D, 2 * DFF])
    g.w_down = din("w_down", [DEPTH, DFF, D])
    g.y = dout("y", [NT, D])
    g.st = dout("st", [NSEG, DEPTH, 2, NH, 128, 128])
    g.projT = scr("projT", [2560, NT], BF16)
    g.gb = scr("gb", [NT, 24], F32)
    g.qT = scr("qT", [DA, NT], BF16)
    g.kT = scr("kT", [DA, NT], BF16)
    g.ktok = scr("ktok", [NT, DA], BF16)
    g.vtok = scr("vtok", [NT, DA], BF16)
    g.uT = scr("uT", [DA, NT], BF16)
    g.ogT = scr("ogT", [DA, NT], BF16)
    g.gtT = scr("gtT", [DFF, NT], BF16)
    g.upT = scr("upT", [DFF, NT], BF16)
    g.aT = scr("aT", [DFF, NT], BF16)
    if debug:
        g.dbg_mod = dout("dbg_mod", [128, 48])

    with ExitStack() as es:
        P = Prog(nc, es)
        g.P = P
        g.nc = nc
        g.consts = P.sb("consts_sb", [128, 512], F32)
        g.identb = P.sb("identb", [128, 128], BF16)
        g.onesb = P.sb("onesb", [128, 128], BF16)
        g.keep = P.sb("keepf", [128, 1], F32)
        g.ML = P.sb("MLp", [128, NSEG, SEG + 2 * PADF], BF16)
        g.MR = P.sb("MRp", [128, NSEG, SEG + 2 * PADF], BF16)
        g.modP = P.sb("modP", [128, 6, 8], F32)
        g.AB = P.sb("AB", [128, 4, 8], F32)
        g.Gm = P.sb("Gm", [128, D], F32)
        g.Gf = P.sb("Gf", [128, D], F32)
        g.scv = P.sb("scv", [128, 8], F32)
        g.vp = P.sb("vecPs", [128, NV], F32)
        g.vb = P.sb("vecBs", [128, 16], F32)
        g.psum = [P.ps("psb%d" % i, [128, 512], F32) for i in range(8)]
        g.ident = g.consts[:, 0:128]
        g.ones = g.consts[:, 128:256]
        g.U = g.consts[0:64, 256:320]
        g.Us = g.consts[0:64, 320:384]
        g.L = g.consts[0:64, 384:448]
        g.Ls = g.consts[0:64, 448:512]

        phase_init(g)
        phases = [(n, globals()["phase_" + n]) for n in ("mod", "p1", "p2a", "p2b", "p3", "p4a", "p4b", "p5", "p6")
                  if ("phase_" + n) in globals()]
        done = False
        for l in range(DEPTH):
            for name, fn in phases:
                fn(g, l)
                if stop_after == name and l == DEPTH - 1:
                    done = True
                    break
            if done:
                break
        P.finish()
    return nc


def phase_init(g):
    P = g.P
    NSEG = g.NSEG
    P.dma("sp", g.consts[:], g.consts_d, writes=["consts"])
    P.dma("sp", g.keep[:], g.keep_d, writes=["keep"])
    P.copy("dve", g.identb[:], g.ident, ["consts"], ["identb"])
    P.copy("dve", g.onesb[:], g.ones, ["consts"], ["onesb"])
    P.dma("sp", g.scv[:], g.cvec, writes=["scv"])
    P.actf(g.scv[:], g.scv[:], AF.Silu, ["scv"], ["scv"])
    with ExitStack() as pes:
        mtmp = _sbf(g, pes)("mtmp", [128, 2, g.NT], F32)
        P.dma("sp", mtmp[:, 0, :], g.cmask[0, :].partition_broadcast(128), writes=["mtmp0"])
        P.dma("sp", mtmp[:, 1, :], g.cmask[1, :].partition_broadcast(128), writes=["mtmp1"])
        for i, M in enumerate((g.ML, g.MR)):
            key = "Mpad%d" % i
            P.memset("dve", M[:], 0.0, [key])
            P.copy("dve", M[:, :, PADF:PADF + SEG], mtmp[:, i, :].rearrange("p (s t) -> p s t", s=NSEG), ["mtmp%d" % i], [key])
            if NSEG > 1:
                P.copy("dve", M[:, 1:, 0:PADF], M[:, :-1, SEG:SEG + PADF], [key], [key])
                P.copy("dve", M[:, :-1, PADF + SEG:], M[:, 1:, PADF:2 * PADF], [key], [key])
        P.barrier()


_UNIQ = [0]


def _sbf(g, pes):
    def f(name, shape, dt):
        _UNIQ[0] += 1
        return pes.enter_context(g.nc.sbuf_tensor("%s_u%d" % (name, _UNIQ[0]), list(shape), dt))
    return f


def phase_mod(g, l):
    P = g.P
    with ExitStack() as pes:
        sb = _sbf(g, pes)
        P.dma("sp", g.vp[:], g.vecP[l], writes=["vp"])
        P.dma("sp", g.vb[:], g.vecB[l].partition_broadcast(128), writes=["vb"])
        P.actf(g.vb[:, 0:8], g.vb[:, 0:8], AF.Exp, ["vb"], ["vb"])
        P.ts("dve", g.vb[:, 0:8], g.vb[:, 0:8], -1.0, None, ALU.mult, None, ["vb"], ["vb"])
        wst = [sb("wmst%d" % i, [128, 8, 1024], BF16) for i in range(3)]
        scb = sb("scvb", [128, 8], BF16)
        P.copy("dve", scb[:], g.scv[:], ["scv"], ["scvb"])
        ps = g.psum[0]
        for j in range(6):
            w = wst[j % 3]
            wkey = "wmst%d" % (j % 3)
            P.dma("pool", w[:], g.w_mod[l, :, j * 1024:(j + 1) * 1024].rearrange("(k p) c -> p k c", p=128), writes=[wkey])
            for cc in range(8):
                for kc in range(8):
                    P.mm(ps[:, j * 8 + cc:j * 8 + cc + 1], w[:, kc, cc * 128:(cc + 1) * 128], scb[:, kc:kc + 1],
                         kc == 0, kc == 7, [wkey, "scvb"], ["ps0"])
        mp = g.modP[:].rearrange("p j k -> p (j k)")
        P.tt("dve", mp, ps[:, 0:48], g.vp[:, OB_BMOD:OB_BMOD + 48], ALU.add, ["ps0", "vp"], ["modP"])
        if hasattr(g, "dbg_mod") and l == 0:
            P.dma("pool", g.dbg_mod, mp, reads=["modP"], writes=["dbg_mod"])
        P.stt("dve", g.AB[:, 0, :], g.modP[:, 1, :], 1.0, g.vp[:, OB_GPM:OB_GPM + 8], ALU.add, ALU.mult, ["modP", "vp"], ["AB"])
        P.copy("dve", g.AB[:, 1, :], g.modP[:, 0, :], ["modP"], ["AB"])
        P.stt("dve", g.AB[:, 2, :], g.modP[:, 4, :], 1.0, g.vp[:, OB_GPF:OB_GPF + 8], ALU.add, ALU.mult, ["modP", "vp"], ["AB"])
        P.copy("dve", g.AB[:, 3, :], g.modP[:, 3, :], ["modP"], ["AB"])
        gcol = sb("gcol", [128, 2, 8], F32)
        P.tt("dve", gcol[:, 0, :], g.modP[:, 2, :], g.vp[:, OB_GQM:OB_GQM + 8], ALU.mult, ["modP", "vp"], ["gcol"])
        P.tt("dve", gcol[:, 1, :], g.modP[:, 5, :], g.vp[:, OB_GQF:OB_GQF + 8], ALU.mult, ["modP", "vp"], ["gcol"])
        dg = [sb("dgm%d" % i, [128, 128], F32) for i in range(2)]
        n = 0
        for gi, G in enumerate((g.Gm, g.Gf)):
            gk = "Gm" if gi == 0 else "Gf"
            for kc in range(8):
                d = dg[n % 2]
                dk = "dgm%d" % (n % 2)
                pb = g.psum[1 + n % 2]
                pk = "ps%d" % (1 + n % 2)
                P.ts("dve", d[:], g.ident, gcol[:, gi, kc:kc + 1], None, ALU.mult, None, ["consts", "gcol"], [dk])
                P.mm(pb[:, 0:128], g.ones, d[:], True, True, ["consts", dk], [pk])
                P.copy("act", G[:, kc * 128:(kc + 1) * 128], pb[:, 0:128], [pk], [gk])
                n += 1
        P.barrier()


def load_weight_bf16(g, sb, dst, dst_key, src2d, KC, N, name):
    P = g.P
    for kc in range(KC):
        P.dma("pool", dst[:, kc, :], src2d[kc * 128:(kc + 1) * 128, :], writes=["%s_%d" % (dst_key, kc)])
    return ["%s_%d" % (dst_key, kc) for kc in range(KC)]


def norm_to_hT(g, x_ap, xkeys, ab_i, hT_out, hkey, bufs, n):
    P = g.P
    stat, skey = bufs["stat"][n % 2], "nh_stat%d" % (n % 2)
    xn, xnkey = bufs["xn"][n % 2], "nh_xn%d" % (n % 2)
    tmp, tkey = bufs["tmp"][n % 2], "nh_tmp%d" % (n % 2)
    junk = bufs["junk"]
    pb = g.psum[6 + n % 2]
    pk = "ps%d" % (6 + n % 2)
    P.memset("dve", stat[:], 0.0, [skey])
    P.actf(junk[:], x_ap, AF.Square, xkeys + [skey], ["nh_junk", skey], accum=stat[:, 0:1])
    yield
    P.actf(stat[:, 1:2], stat[:, 0:1], AF.Sqrt, [skey], [skey], bias=EPS, scale=1.0 / D)
    P.op("dve", lambda e: e.reciprocal(stat[:, 2:3], stat[:, 1:2]), [skey], [skey])
    yield
    P.actf(xn[:], x_ap, AF.Identity, xkeys + [skey], [xnkey], scale=stat[:, 2:3])
    yield
    pv = pb[:].bitcast(BF16)
    for kc in range(8):
        P.tr(pv[:, kc * 128:(kc + 1) * 128], xn[:, kc * 128:(kc + 1) * 128], g.identb[:], [xnkey, "identb"], [pk])
        if kc == 3:
            yield
    yield
    pv3 = pv.rearrange("p (k t) -> p k t", k=8)
    P.tt("dve", tmp[:], pv3, bc_last(g.AB[:, ab_i, :], 128), ALU.mult, [pk, "AB"], [tkey])
    yield
    P.tt("dve", hT_out, tmp[:], bc_last(g.AB[:, ab_i + 1, :], 128), ALU.add, [tkey, "AB"], [hkey])
    yield


def interleave(main, side, ratio):
    acc = 0.0
    side_done = side is None
    for _ in main:
        acc += ratio
        while acc >= 1.0 and not side_done:
            acc -= 1.0
            try:
                next(side)
            except StopIteration:
                side_done = True
    while not side_done:
        try:
            next(side)
        except StopIteration:
            side_done = True


def nh_bufs(sb):
    return {
        "stat": [sb("nh_stat%d" % i, [128, 4], F32) for i in range(2)],
        "xn": [sb("nh_xn%d" % i, [128, D], BF16) for i in range(2)],
        "tmp": [sb("nh_tmp%d" % i, [128, 8, 128], F32) for i in range(2)],
        "junk": sb("nh_junk", [128, D], BF16),
    }


def phase_p1(g, l):
    P = g.P
    src = g.x_in if l == 0 else g.y
    srckey = "x_dram" if l == 0 else "y_dram"
    with ExitStack() as pes:
        sb = _sbf(g, pes)
        wbf = sb("p1_w", [128, 8, INC], BF16)
        WK = load_weight_bf16(g, sb, wbf, "p1_w", g.w_in[l], 8, INC, "p1w")
        nb = nh_bufs(sb)
        xt = [sb("p1_x%d" % i, [128, 4, D], F32) for i in range(2)]
        hT = [sb("p1_hT%d" % i, [128, 8, 512], BF16) for i in range(2)]
        stg = [sb("p1_stg%d" % i, [128, 512], BF16) for i in range(4)]
        sig = [sb("p1_sig%d" % i, [128, 512], F32) for i in range(2)]
        gbt = [sb("p1_gbt%d" % i, [128, 24], F32) for i in range(2)]
        t8 = [sb("p1_t8%d" % i, [128, 8], F32) for i in range(2)]
        cnt = {"nsub": 0, "nst": 0}

        def norm_tile(T):
            x, xk = xt[T % 2], "p1_x%d" % (T % 2)
            h, hk = hT[T % 2], "p1_hT%d" % (T % 2)
            P.dma("sp", x[:], src[T * 512:(T + 1) * 512, :].rearrange("(s p) f -> p s f", p=128), reads=[], writes=[xk])
            for s in range(4):
                yield from norm_to_hT(g, x[:, s, :], [xk], 0, h[:, :, s * 128:(s + 1) * 128], hk, nb, cnt["nsub"])
                cnt["nsub"] += 1

        def mm_tile(T):
            h, hk = hT[T % 2], "p1_hT%d" % (T % 2)
            tok = slice(T * 512, (T + 1) * 512)
            npb = 0
            seq = [("c", cc) for cc in range(16)]
            for i in range(4):
                seq += [("gb", i), ("ga", i)]
            for kind, i in seq:
                col0 = i * 128 if kind == "c" else (2576 + 128 * i if kind == "gb" else 2064 + 128 * i)
                pb = g.psum[npb % 4]
                pk = "ps%d" % (npb % 4)
                npb += 1
                for kc in range(8):
                    P.mm(pb[:, :], wbf[:, kc, col0:col0 + 128], h[:, kc, :], kc == 0, kc == 7, [WK[kc], hk], [pk])
                if kind == "gb":
                    sg, sgk = sig[i % 2], "p1_sig%d" % (i % 2)
                    P.actf(sg[:], pb[:, :], AF.Sigmoid, [pk], [sgk])
                    yield
                    continue
                st_, stk = stg[cnt["nst"] % 4], "p1_stg%d" % (cnt["nst"] % 4)
                cnt["nst"] += 1
                if kind == "ga":
                    P.tt("dve", st_[:], pb[:, :], sig[i % 2][:], ALU.mult, [pk, "p1_sig%d" % (i % 2)], [stk])
                    row0 = 2048 + i * 128
                elif i >= 12:
                    P.actf(st_[:], pb[:, :], AF.Silu, [pk], [stk])
                    row0 = i * 128
                else:
                    P.copy("act" if i % 2 == 0 else "dve", st_[:], pb[:, :], [pk], [stk])
                    row0 = i * 128
                P.dma("pool", g.projT[row0:row0 + 128, tok], st_[:], reads=[stk], writes=[])
                yield
            for s in range(4):
                pb = g.psum[4 + s % 2]
                pk = "ps%d" % (4 + s % 2)
                for kc in range(8):
                    P.mm(pb[:, 0:16], h[:, kc, s * 128:(s + 1) * 128], wbf[:, kc, 2048:2064], kc == 0, kc == 7, [WK[kc], hk], [pk])
                gt_, gk = gbt[s % 2], "p1_gbt%d" % (s % 2)
                t, tk = t8[s % 2], "p1_t8%d" % (s % 2)
                P.tt("dve", t[:], pb[:, 0:8], g.vb[:, 8:16], ALU.add, [pk, "vb"], [tk])
                P.actf(t[:], t[:], AF.Exp, [tk], [tk])
                P.actf(t[:], t[:], AF.Ln, [tk], [tk], bias=1.0)
                P.tt("dve", gt_[:, 0:8], t[:], g.vb[:, 0:8], ALU.mult, [tk, "vb"], [gk])
                P.actf(gt_[:, 8:16], pb[:, 8:16], AF.Sigmoid, [pk], [gk])
                P.ts("dve", gt_[:, 16:24], gt_[:, 8:16], -1.0, None, ALU.mult, None, [gk], [gk])
                P.dma("pool", g.gb[T * 512 + s * 128:T * 512 + (s + 1) * 128, :], gt_[:], reads=[gk], writes=[])
                yield

        for _ in norm_tile(0):
            pass
        for T in range(g.NTT):
            interleave(mm_tile(T), norm_tile(T + 1) if T + 1 < g.NTT else None, 1.2)
        P.barrier()


def _pp(v):
    return np.ascontiguousarray(np.asarray(v, np.float32).reshape(-1, 128).T)


def make_consts():
    c = np.zeros((128, 512), np.float32)
    c[:, 0:128] = np.eye(128, dtype=np.float32)
    c[:, 128:256] = 1.0
    i = np.arange(64)
    c[0:64, 256:320] = (i[:, None] <= i[None, :])
    c[0:64, 320:384] = (i[:, None] < i[None, :])
    c[0:64, 384:448] = (i[:, None] >= i[None, :])
    c[0:64, 448:512] = (i[:, None] > i[None, :])
    return c


def make_shared(inp, DEPTH):
    f = lambda a: np.asarray(a, np.float32)
    packs = []
    for grid in (True, False):
        vp = np.zeros((DEPTH, 128, NV), np.float32)
        for l in range(DEPTH):
            vp[l, :, OB_BMOD:OB_BMOD + 48] = _pp(f(inp["b_mod"])[l])
            vp[l, :, OB_GPM:OB_GPM + 8] = _pp(f(inp["g_pre_mix"])[l])
            vp[l, :, OB_GQM:OB_GQM + 8] = _pp(f(inp["g_post_mix"])[l])
            vp[l, :, OB_GPF:OB_GPF + 8] = _pp(f(inp["g_pre_ffn"])[l])
            vp[l, :, OB_GQF:OB_GQF + 8] = _pp(f(inp["g_post_ffn"])[l])
            vp[l, :, OB_CQ:OB_CQ + 60] = f(inp["conv_qkv"])[l].reshape(5, 12, 128).transpose(2, 1, 0).reshape(128, 60)
            vp[l, :, OB_CCF:OB_CCF + 124] = f(inp["conv_cf"])[l].reshape(31, 4, 128).transpose(2, 1, 0).reshape(128, 124)
            vp[l, :, OB_BCF:OB_BCF + 4] = _pp(f(inp["b_conv_cf"])[l])
            vp[l, :, OB_LNG:OB_LNG + 4] = _pp(f(inp["ln_cf_g"])[l])
            vp[l, :, OB_LNB:OB_LNB + 4] = _pp(f(inp["ln_cf_b"])[l])
            vp[l, :, OB_GON:OB_GON + 1] = f(inp["g_onorm"])[l].reshape(128, 1)
            vp[l, :, OB_BFF:OB_BFF + 22] = _pp(f(inp["b_conv_ffn"])[l])
            cf = f(inp["conv_ffn"])[l].reshape(9, DFF)
            if not grid:
                sel = np.zeros_like(cf)
                sel[3:6] = cf[3:6]
                cf = sel
            vp[l, :, OB_CFFN:OB_CFFN + 198] = cf.reshape(9, 22, 128).transpose(2, 1, 0).reshape(128, 198)
        packs.append(vp)
    vb = np.concatenate([f(inp["a_log"]).reshape(DEPTH, 8), f(inp["dt_bias"]).reshape(DEPTH, 8)], axis=1)
    return packs[0], packs[1], np.ascontiguousarray(vb)


def make_core_map(inp, shared, NT, DEPTH, grid, x_tok, cvec, s0):
    vp_grid, vp_seq, vb = shared
    t = np.arange(NT)
    if grid:
        cm = np.stack([(t % 64 != 63), (t % 64 != 0)]).astype(np.float32)
        keep = np.ones((128, 1), np.float32)
    else:
        cm = np.ones((2, NT), np.float32)
        keep = np.zeros((128, 1), np.float32)
    m = {
        "x": np.ascontiguousarray(x_tok, np.float32),
        "cvec": _pp(cvec),
        "vecP": vp_grid if grid else vp_seq,
        "vecB": vb,
        "s0": np.ascontiguousarray(s0, np.float32),
        "keep": keep,
        "cmask": cm,
        "consts": make_consts(),
    }
    for k in ("w_mod", "w_in", "w_out", "w_up", "w_down"):
        m[k] = np.ascontiguousarray(np.asarray(inp[k], np.float32)[:DEPTH])
    return m


def _halo(g, cin, key, pad):
    P = g.P
    if g.NSEG > 1:
        P.ts("dve", cin[:, 1:, 0:pad], cin[:, :-1, SEG:SEG + pad], g.keep[:, 0:1], None, ALU.mult, None, [key, "keep"], [key])
        P.ts("dve", cin[:, :-1, pad + SEG:pad + SEG + pad], cin[:, 1:, pad:2 * pad], g.keep[:, 0:1], None, ALU.mult, None, [key, "keep"], [key])


def phase_p2a(g, l):
    P = g.P
    NT, NSEG, NTT, NB = g.NT, g.NSEG, g.NTT, g.NB
    with ExitStack() as pes:
        sb = _sbf(g, pes)
        cin = [sb("p2_cin%d" % i, [128, NSEG, SEG + 2 * PADQ], BF16) for i in range(2)]
        acc = [sb("p2_acc%d" % i, [128, NT], F32) for i in range(2)]
        sq = sb("p2_sq", [128, NT], F32)
        rs = [sb("p2_rs%d" % i, [128, 512], F32) for i in range(2)]
        obf = [sb("p2_obf%d" % i, [128, NT], BF16) for i in range(2)]
        tst = [sb("p2_tst%d" % i, [128, 8, 128], BF16) for i in range(2)]
        dgq = [sb("p2_dg%d" % i, [128, 5, 128], BF16) for i in range(2)]
        for i in range(2):
            P.memset("dve", cin[i][:], 0.0, ["p2_cin%d" % i])
        cnt = {"ntr": 0}

        def stage1(cc):
            ci, ck = cin[cc % 2], "p2_cin%d" % (cc % 2)
            a, ak = acc[cc % 2], "p2_acc%d" % (cc % 2)
            ob, obk = obf[cc % 2], "p2_obf%d" % (cc % 2)
            P.dma("sp", ci[:, :, PADQ:PADQ + SEG], g.projT[cc * 128:(cc + 1) * 128, :].rearrange("p (s t) -> p s t", s=NSEG),
                  reads=["projT"], writes=[ck])
            _halo(g, ci, ck, PADQ)
            dg, dgk = dgq[cc % 2], "p2_dg%d" % (cc % 2)
            P.tt("pool", dg[:], bc_mid(g.identb[:], 5), bc_last(g.vp[:, OB_CQ + cc * 5:OB_CQ + cc * 5 + 5], 128), ALU.mult,
                 ["identb", "vp"], [dgk])
            yield
            for T in range(NTT):
                pb, pk = g.psum[4 + T % 4], "ps%d" % (4 + T % 4)
                o3 = pb[:, :].rearrange("p (s t) -> p s t", s=2)
                for j in range(5):
                    P.mm(o3, dg[:, j, :], ci[:, 2 * T:2 * T + 2, j:j + SEG], j == 0, j == 4, [dgk, ck], [pk])
                tokc = slice(T * 512, (T + 1) * 512)
                if cc >= 8:
                    P.actf(ob[:, tokc], pb[:, :], AF.Silu, [pk], [obk])
                else:
                    P.actf(a[:, tokc], pb[:, :], AF.Silu, [pk], [ak])
                yield

        def stage2(cc):
            a, ak = acc[cc % 2], "p2_acc%d" % (cc % 2)
            ob, obk = obf[cc % 2], "p2_obf%d" % (cc % 2)
            head = cc % 4
            if cc < 8:
                P.tt("dve", sq[:], a[:], a[:], ALU.mult, [ak], ["p2_sq"])
                yield
                for T in range(NTT):
                    tok = slice(T * 512, (T + 1) * 512)
                    pb, pk = g.psum[T % 2], "ps%d" % (T % 2)
                    r, rk = rs[T % 2], "p2_rs%d" % (T % 2)
                    P.mm(pb[:, :], g.ones, sq[:, tok], True, True, ["consts", "p2_sq"], [pk])
                    P.actf(r[:], pb[:, :], AF.Sqrt, [pk], [rk], bias=EPS)
                    P.op("dve", lambda e, r=r: e.reciprocal(r[:], r[:]), [rk], [rk])
                    if cc < 4:
                        P.stt("dve", ob[:, tok], a[:, tok], float(128 ** -0.5), r[:], ALU.mult, ALU.mult, [ak, rk], [obk])
                    else:
                        P.tt("dve", ob[:, tok], a[:, tok], r[:], ALU.mult, [ak, rk], [obk])
                    yield
                dst = g.qT if cc < 4 else g.kT
                P.dma("pool", dst[head * 128:(head + 1) * 128, :], ob[:], reads=[obk], writes=[])
            if cc >= 4:
                dstt = g.ktok if cc < 8 else g.vtok
                dv = dstt.rearrange("(b p) f -> p b f", p=128)
                for b0 in range(0, NB, 8):
                    nbk = min(8, NB - b0)
                    ntr = cnt["ntr"]
                    cnt["ntr"] += 1
                    pb, pk = g.psum[2 + ntr % 2], "ps%d" % (2 + ntr % 2)
                    ts_, tk = tst[ntr % 2], "p2_tst%d" % (ntr % 2)
                    pv = pb[:].bitcast(BF16)
                    for b in range(nbk):
                        P.tr(pv[:, b * 128:(b + 1) * 128], ob[:, (b0 + b) * 128:(b0 + b + 1) * 128], g.identb[:], [obk, "identb"], [pk])
                    P.copy("act", ts_[:, 0:nbk, :], pv[:, 0:nbk * 128].rearrange("p (b f) -> p b f", b=nbk), [pk], [tk])
                    P.dma("pool", dv[:, b0:b0 + nbk, head * 128:(head + 1) * 128], ts_[:, 0:nbk, :], reads=[tk], writes=[])
                    yield

        for _ in stage1(0):
            pass
        for cc in range(12):
            interleave(stage2(cc), stage1(cc + 1) if cc + 1 < 12 else None, 1.0)
        P.barrier()


def phase_p2b(g, l):
    P = g.P
    NT, NSEG, NTT = g.NT, g.NSEG, g.NTT
    with ExitStack() as pes:
        sb = _sbf(g, pes)
        cin = [sb("p2b_cin%d" % i, [128, NSEG, SEG + 2 * PADC], BF16) for i in range(2)]
        ucv = sb("p2b_ucv", [128, 4, NT], F32)
        sq4 = [sb("p2b_sq%d" % i, [128, 4, 512], F32) for i in range(2)]
        mean = [sb("p2b_mean%d" % i, [128, 512], F32) for i in range(2)]
        msq = [sb("p2b_msq%d" % i, [128, 512], F32) for i in range(2)]
        rstd = [sb("p2b_rstd%d" % i, [128, 512], F32) for i in range(2)]
        t1 = [sb("p2b_t1%d" % i, [128, 512], F32) for i in range(2)]
        stg = [sb("p2b_stg%d" % i, [128, 512], BF16) for i in range(2)]
        dgc = [sb("p2b_dg%d" % i, [128, 31, 128], BF16) for i in range(2)]
        for i in range(2):
            P.memset("dve", cin[i][:], 0.0, ["p2b_cin%d" % i])
        for c in range(4):
            ci, ck = cin[c % 2], "p2b_cin%d" % (c % 2)
            uk = "p2b_ucv%d" % c
            P.dma("sp", ci[:, :, PADC:PADC + SEG], g.projT[2048 + c * 128:2048 + (c + 1) * 128, :].rearrange("p (s t) -> p s t", s=NSEG),
                  reads=["projT"], writes=[ck])
            _halo(g, ci, ck, PADC)
            dg, dgk = dgc[c % 2], "p2b_dg%d" % (c % 2)
            P.tt("pool", dg[:], bc_mid(g.identb[:], 31), bc_last(g.vp[:, OB_CCF + c * 31:OB_CCF + c * 31 + 31], 128), ALU.mult,
                 ["identb", "vp"], [dgk])
            for T in range(NTT):
                pb, pk = g.psum[4 + T % 4], "ps%d" % (4 + T % 4)
                o3 = pb[:, :].rearrange("p (s t) -> p s t", s=2)
                for j in range(31):
                    P.mm(o3, dg[:, j, :], ci[:, 2 * T:2 * T + 2, j:j + SEG], j == 0, j == 30, [dgk, ck], [pk])
                P.actf(ucv[:, c, T * 512:(T + 1) * 512], pb[:, :], AF.Identity, [pk, "vp"], [uk],
                       bias=g.vp[:, OB_BCF + c:OB_BCF + c + 1])
        UK = ["p2b_ucv%d" % c for c in range(4)]
        n = 0
        for T in range(NTT):
            tok = slice(T * 512, (T + 1) * 512)
            i2 = T % 2
            P.tt("pool", sq4[i2][:], ucv[:, :, tok], ucv[:, :, tok], ALU.mult, UK, ["p2b_sq%d" % i2])
            p1, p1k = g.psum[0 + 2 * i2], "ps%d" % (0 + 2 * i2)
            p2, p2k = g.psum[1 + 2 * i2], "ps%d" % (1 + 2 * i2)
            for c in range(4):
                P.mm(p1[:, :], g.ones, ucv[:, c, tok], c == 0, c == 3, ["consts"] + UK, [p1k])
            for c in range(4):
                P.mm(p2[:, :], g.ones, sq4[i2][:, c, :], c == 0, c == 3, ["consts", "p2b_sq%d" % i2], [p2k])
            mk, qk, rk = "p2b_mean%d" % i2, "p2b_msq%d" % i2, "p2b_rstd%d" % i2
            P.actf(mean[i2][:], p1[:, :], AF.Identity, [p1k], [mk], scale=1.0 / DA)
            P.tt("dve", msq[i2][:], mean[i2][:], mean[i2][:], ALU.mult, [mk], [qk])
            P.stt("dve", rstd[i2][:], p2[:, :], 1.0 / DA, msq[i2][:], ALU.mult, ALU.subtract, [p2k, qk], [rk])
            P.actf(rstd[i2][:], rstd[i2][:], AF.Sqrt, [rk], [rk], bias=EPS)
            P.op("dve", lambda e, r=rstd[i2]: e.reciprocal(r[:], r[:]), [rk], [rk])
            for c in range(4):
                t, tk = t1[n % 2], "p2b_t1%d" % (n % 2)
                s_, sk = stg[n % 2], "p2b_stg%d" % (n % 2)
                n += 1
                P.tt("dve", t[:], ucv[:, c, tok], mean[i2][:], ALU.subtract, UK + [mk], [tk])
                P.tt("dve", t[:], t[:], rstd[i2][:], ALU.mult, [tk, rk], [tk])
                P.actf(s_[:], t[:], AF.Silu, [tk, "vp"], [sk], bias=g.vp[:, OB_LNB + c:OB_LNB + c + 1],
                       scale=g.vp[:, OB_LNG + c:OB_LNG + c + 1])
                P.dma("pool", g.uT[c * 128:(c + 1) * 128, tok], s_[:], reads=[sk], writes=[])
        P.barrier()


def phase_p3(g, l):
    P = g.P
    NT, NTT, NCH, NSEG = g.NT, g.NTT, g.NCH, g.NSEG
    CPS = SEG // CH
    ps = g.psum
    id64 = g.consts[0:64, 0:64]
    ones64 = g.consts[0:64, 128:256]
    idb64 = g.identb[0:64, 0:64]
    with ExitStack() as pes:
        sb = _sbf(g, pes)
        oT = sb("p3_oT", [128, NH, NT], F32)
        pes2 = ExitStack()
        sb_outer = sb
        sb = _sbf(g, pes2)
        S = sb("p3_S", [128, NH, 128], F32)
        Sb = sb("p3_Sb", [128, NH, 128], BF16)
        t2 = sb("p3_t2", [128, NH, 128], F32)
        NSET = 3
        kTb = [sb("p3_kTb%d" % i, [128, NH, SEG], BF16) for i in range(NSET)]
        qTb = [sb("p3_qTb%d" % i, [128, NH, SEG], BF16) for i in range(NSET)]
        ktb = [sb("p3_ktb%d" % i, [64, CPS, 512], BF16) for i in range(NSET)]
        vtb = [sb("p3_vtb%d" % i, [64, CPS, 512], BF16) for i in range(NSET)]
        gbb = [sb("p3_gbb%d" % i, [64, CPS, 24], F32) for i in range(NSET)]

        def two(name, shape, dt):
            return [sb("%s%d" % (name, i), shape, dt) for i in range(2)]
        smK = two("p3_sm", [128, CPS, 24], F32)
        YK = two("p3_Y", [64, CPS, NH, 64], BF16)
        qkdK = two("p3_qkd", [64, CPS, NH, 64], BF16)
        qgTK = two("p3_qgT", [128, CPS, NH, 64], BF16)
        kdecK = two("p3_kdec", [64, CPS, NH, 128], BF16)
        Pq = [sb("p3_P%d" % i, [64, CPS, NH, 64], BF16) for i in range(2)]
        PTq = [sb("p3_PT%d" % i, [64, CPS, NH, 64], BF16) for i in range(2)]
        Gtri4 = sb("p3_Gtri4", [64, CPS, NH, 64], F32)
        Dg4 = sb("p3_Dg4", [64, CPS, NH, 64], F32)
        dec = two("p3_dec", [64, NH, 64], F32)
        decm = two("p3_decm", [64, NH, 64], F32)
        decs = two("p3_decs", [64, NH, 64], F32)
        tmpk = two("p3_tmpk", [64, NH, 64], F32)
        t1 = two("p3_t1", [64, NH, 128], F32)
        nr = two("p3_nr", [64, NH, 128], BF16)
        vnew = two("p3_vnew", [64, NH, 128], BF16)

        def v4(ap):
            return ap.rearrange("p (h c) -> p h c", h=NH)

        seq = [(d, B) for d in range(2) for B in (range(NSEG) if d == 0 else range(NSEG - 1, -1, -1))]
        cnt = {"tr": 0}

        def load_block(i):
            d, B = seq[i]
            st_ = i % NSET
            tok = slice(B * SEG, (B + 1) * SEG)
            P.dma("sp", kTb[st_][:], g.kT.rearrange("(h p) t -> p h t", p=128)[:, :, tok], reads=["kT"], writes=["p3_kTb%d" % st_])
            P.dma("sp", qTb[st_][:], g.qT.rearrange("(h p) t -> p h t", p=128)[:, :, tok], reads=["qT"], writes=["p3_qTb%d" % st_])
            P.dma("sp", ktb[st_][:], g.ktok[tok, :].rearrange("(c p) f -> p c f", p=64), reads=["ktok"], writes=["p3_ktb%d" % st_])
            P.dma("sp", vtb[st_][:], g.vtok[tok, :].rearrange("(c p) f -> p c f", p=64), reads=["vtok"], writes=["p3_vtb%d" % st_])
            P.dma("sp", gbb[st_][:], g.gb[tok, :].rearrange("(c p) j -> p c j", p=64), reads=["gb"], writes=["p3_gbb%d" % st_])

        def pre_gen(i):
            d, B = seq[i]
            st_ = i % NSET
            ks = i % 2
            TRI = g.U if d == 0 else g.L
            STRICT = g.Ls if d == 0 else g.Us
            MI = g.U if d == 0 else g.L
            MS = g.Us if d == 0 else g.Ls
            kk, qk_, ktk, vtk, gbk = ["p3_%s%d" % (n, st_) for n in ("kTb", "qTb", "ktb", "vtb", "gbb")]
            if i + 1 < len(seq):
                load_block(i + 1)
            KK = lambda n, ci: "p3k_%s_%d_%d" % (n, ks, ci)
            backs = []
            for ci in range(CPS):
                P.tt("pool", Gtri4[:, ci], bc_mid(TRI, NH), bc_last(gbb[st_][:, ci, d * 4:d * 4 + 4], 64), ALU.mult, ["consts", gbk],
                     ["p3_Gtri4_%d" % ci])

            def emit_back(item):
                ci_, AT_, A_, atk_, ak_ = item
                atr = ps[0][0:64, 264:392].bitcast(BF16)
                for h in range(NH):
                    P.tr(atr[:, h * 64:(h + 1) * 64], AT_[:, h, :], idb64, [atk_, "identb"], ["ps0"])
                P.copy("act", A_, v4(atr), ["ps0"], [ak_])
                P.tt("pool", YK[ks][:, ci_], AT_, bc_mid(id64, NH), ALU.add, [atk_, "consts"], [KK("Y", ci_)])

            for ci in range(CPS):
                par = cnt["tr"] % 2
                cnt["tr"] += 1
                TK = lambda n: "p3_%s%d" % (n, par)
                gv = gbb[st_][:, ci, d * 4:d * 4 + 4]
                nbv = gbb[st_][:, ci, 16 + d * 4:16 + d * 4 + 4]
                kTc = kTb[st_][:, :, ci * 64:(ci + 1) * 64]
                qTc = qTb[st_][:, :, ci * 64:(ci + 1) * 64]
                smp = smK[ks][:, ci, :]
                smk = KK("sm", ci)
                egc, etot, totsb, tmg, edec = smp[0:64, 0:4], smp[:, 4:8], smp[:, 8:12], smp[0:64, 12:16], smp[0:64, 16:20]
                P.mm(ps[0][0:64, 0:4], TRI, gv, True, True, ["consts", gbk], ["ps0"])
                P.mm(ps[0][:, 4:8], ones64, gv, True, True, ["consts", gbk], ["ps0"])
                P.mm(ps[0][0:64, 8:264], STRICT, Gtri4[:, ci].rearrange("p h c -> p (h c)"), True, True, ["consts", "p3_Gtri4_%d" % ci], ["ps0"])
                P.actf(egc, ps[0][0:64, 0:4], AF.Exp, ["ps0"], [smk])
                P.actf(etot, ps[0][:, 4:8], AF.Exp, ["ps0"], [smk])
                P.copy("act", totsb, ps[0][:, 4:8], ["ps0"], [smk])
                P.actf(dec[par][:], v4(ps[0][0:64, 8:264]), AF.Exp, ["ps0"], [TK("dec")])
                P.tt("dve", tmg, smp[0:64, 8:12], ps[0][0:64, 0:4], ALU.subtract, [smk, "ps0"], [smk])
                P.actf(edec, tmg, AF.Exp, [smk], [smk])
                P.tt("dve", decm[par][:], dec[par][:], bc_mid(MI, NH), ALU.mult, [TK("dec"), "consts"], [TK("decm")])
                P.tt("pool", decs[par][:], dec[par][:], bc_mid(MS, NH), ALU.mult, [TK("dec"), "consts"], [TK("decs")])
                yield
                for h in range(NH):
                    P.mm(ps[1][0:64, h * 64:(h + 1) * 64], kTc[:, h, :], kTc[:, h, :], True, True, [kk], ["ps1"])
                for h in range(NH):
                    P.mm(ps[1][0:64, 256 + h * 64:256 + (h + 1) * 64], kTc[:, h, :], qTc[:, h, :], True, True, [kk, qk_], ["ps1"])
                AT = PTq[0][:, ci]
                A = Pq[0][:, ci]
                atk, ak = "p3_PT0_%d" % ci, "p3_P0_%d" % ci
                P.tt("dve", tmpk[par][:], v4(ps[1][0:64, 0:256]), bc_last(nbv, 64), ALU.mult, ["ps1", gbk], [TK("tmpk")])
                P.tt("dve", qkdK[ks][:, ci], v4(ps[1][0:64, 256:512]), decm[par][:], ALU.mult, ["ps1", TK("decm")], [KK("qkd", ci)])
                P.tt("dve", AT, tmpk[par][:], decs[par][:], ALU.mult, [TK("tmpk"), TK("decs")], [atk])
                backs.append((ci, AT, A, atk, ak))
                if len(backs) >= 3:
                    emit_back(backs.pop(0))
                yield
            while backs:
                emit_back(backs.pop(0))
                yield
            for k in range(6):
                a_, b_ = k % 2, (k + 1) % 2
                for ci in range(CPS):
                    bA, bB = (3, 4) if ci % 2 == 0 else (6, 7)
                    bAk, bBk = "ps%d" % bA, "ps%d" % bB
                    Pc, PTc = Pq[a_][:, ci], PTq[a_][:, ci]
                    Pn, PTn = Pq[b_][:, ci], PTq[b_][:, ci]
                    pck, ptck = "p3_P%d_%d" % (a_, ci), "p3_PT%d_%d" % (a_, ci)
                    pnk, ptnk = "p3_P%d_%d" % (b_, ci), "p3_PT%d_%d" % (b_, ci)
                    Yc = YK[ks][:, ci]
                    if k <= 4:
                        for h in range(NH):
                            P.mm(ps[bA][0:64, h * 64:(h + 1) * 64], PTc[:, h, :], Pc[:, h, :], True, True, [pck, ptck], [bAk])
                    if k < 4:
                        for h in range(NH):
                            P.mm(ps[bB][0:64, 256 + h * 64:256 + (h + 1) * 64], Pc[:, h, :], PTc[:, h, :], True, True, [pck, ptck], [bBk])
                    if k >= 1:
                        for h in range(NH):
                            P.mm(ps[bB][0:64, h * 64:(h + 1) * 64], Pc[:, h, :], Yc[:, h, :], True, True, [pck, KK("Y", ci)], [bBk])
                    if k <= 4:
                        P.copy("act", Pn, v4(ps[bA][0:64, 0:256]), [bAk], [pnk])
                    if k < 4:
                        P.copy("act", PTn, v4(ps[bB][0:64, 256:512]), [bBk], [ptnk])
                    if k >= 1:
                        P.tt("dve", Yc, Yc, v4(ps[bB][0:64, 0:256]), ALU.add, [KK("Y", ci), bBk], [KK("Y", ci)])
                    yield
            for ci in range(CPS):
                smp = smK[ks][:, ci, :]
                smk = KK("sm", ci)
                egc, edec = smp[0:64, 0:4], smp[0:64, 16:20]
                kc = ktb[st_][:, ci, :].rearrange("p (h f) -> p h f", h=NH)
                P.tt("pool", Dg4[:, ci], bc_mid(id64, NH), bc_last(egc, 64), ALU.mult, ["consts", smk], ["p3_Dg4_%d" % ci])
                P.tt("pool", kdecK[ks][:, ci], kc, bc_last(edec, 128), ALU.mult, [ktk, smk], [KK("kdec", ci)])
            yield
            for ci in range(CPS):
                qTc = qTb[st_][:, :, ci * 64:(ci + 1) * 64]
                P.mm(ps[1][:, 0:256], ones64, Dg4[:, ci].rearrange("p h c -> p (h c)"), True, True, ["consts", "p3_Dg4_%d" % ci], ["ps1"])
                P.tt("dve", qgTK[ks][:, ci], qTc, v4(ps[1][:, 0:256]), ALU.mult, [qk_, "ps1"], [KK("qgT", ci)])
                yield

        def chain_gen(i):
            d, B = seq[i]
            st_ = i % NSET
            ks = i % 2
            kk, vtk, gbk = "p3_kTb%d" % st_, "p3_vtb%d" % st_, "p3_gbb%d" % st_
            KK = lambda n, ci: "p3k_%s_%d_%d" % (n, ks, ci)
            first_of_dir = (B == 0) if d == 0 else (B == NSEG - 1)
            if first_of_dir:
                P.dma("sp", S[:], g.s0[l, d].rearrange("h k v -> k h v"), writes=["p3_S"])
                P.copy("act", Sb[:], S[:], ["p3_S"], ["p3_Sb"])
            cis = list(range(CPS)) if d == 0 else list(range(CPS - 1, -1, -1))
            for n_, ci in enumerate(cis):
                c = B * CPS + ci
                par = (i * CPS + n_) % 2
                TK = lambda n: "p3_%s%d" % (n, par)
                smp = smK[ks][:, ci, :]
                smk = KK("sm", ci)
                egc, etot = smp[0:64, 0:4], smp[:, 4:8]
                nbv = gbb[st_][:, ci, 16 + d * 4:16 + d * 4 + 4]
                kTc = kTb[st_][:, :, ci * 64:(ci + 1) * 64]
                vc = vtb[st_][:, ci, :].rearrange("p (h f) -> p h f", h=NH)
                Yc, qkdc, qgTc, kdc = YK[ks][:, ci], qkdK[ks][:, ci], qgTK[ks][:, ci], kdecK[ks][:, ci]
                P.tt("pool", t2[:], S[:], bc_last(etot, 128), ALU.mult, ["p3_S", smk], ["p3_t2"])
                for h in range(NH):
                    P.mm(ps[5][0:64, h * 128:(h + 1) * 128], kTc[:, h, :], Sb[:, h, :], True, True, [kk, "p3_Sb"], ["ps5"])
                yield
                for h in range(NH):
                    P.stt("dve", nr[par][:, h, :], ps[5][0:64, h * 128:(h + 1) * 128], egc[:, h:h + 1], vc[:, h, :], ALU.mult, ALU.subtract,
                          ["ps5", smk, vtk], [TK("nr")])
                yield
                for h in range(NH):
                    P.mm(ps[5][0:64, h * 128:(h + 1) * 128], Yc[:, h, :], nr[par][:, h, :], True, True, [KK("Y", ci), TK("nr")], ["ps5"])
                yield
                P.tt("dve", vnew[par][:], v4(ps[5][0:64, :]), bc_last(nbv, 128), ALU.mult, ["ps5", gbk], [TK("vnew")])
                yield
                for h in range(NH):
                    o_ps = ps[2][:, 256 + h * 64:256 + (h + 1) * 64]
                    P.mm(o_ps, Sb[:, h, :], qgTc[:, h, :], True, False, ["p3_Sb", KK("qgT", ci)], ["ps2"])
                    P.mm(o_ps, vnew[par][:, h, :], qkdc[:, h, :], False, True, [TK("vnew"), KK("qkd", ci)], ["ps2"])
                for h in range(NH):
                    P.mm(ps[5][:, h * 128:(h + 1) * 128], kdc[:, h, :], vnew[par][:, h, :], True, True, [KK("kdec", ci), TK("vnew")], ["ps5"])
                o_sl = oT[:, :, c * 64:(c + 1) * 64]
                if d == 0:
                    P.copy("act", o_sl, v4(ps[2][:, 256:512]), ["ps2"], ["p3_oT"])
                else:
                    P.tt("dve", o_sl, o_sl, v4(ps[2][:, 256:512]), ALU.add, ["ps2", "p3_oT"], ["p3_oT"])
                P.tt("dve", S[:], t2[:], v4(ps[5][:, :]), ALU.add, ["p3_t2", "ps5"], ["p3_S"])
                if n_ == CPS - 1:
                    P.dma("pool", g.st[B, l, d].rearrange("h k v -> k h v"), S[:], reads=["p3_S"], writes=[])
                    last = (B == NSEG - 1) if d == 0 else (B == 0)
                    if not last:
                        P.ts("dve", S[:], S[:], g.keep[:, 0:1], None, ALU.mult, None, ["p3_S", "keep"], ["p3_S"])
                P.copy("act", Sb[:], S[:], ["p3_S"], ["p3_Sb"])
                yield

        load_block(0)
        for _ in pre_gen(0):
            pass
        for i in range(len(seq)):
            ch = chain_gen(i)
            pr = pre_gen(i + 1) if i + 1 < len(seq) else iter(())
            ch_done = pr_done = False
            while not (ch_done and pr_done):
                if not ch_done:
                    try:
                        next(ch)
                    except StopIteration:
                        ch_done = True
                for _ in range(4):
                    if not pr_done:
                        try:
                            next(pr)
                        except StopIteration:
                            pr_done = True
        P.barrier()
        pes2.close()
        sb = sb_outer
        zs = two("p3_zs", [128, NT], BF16)
        oo = two("p3_oo", [128, NT], F32)
        rs = two("p3_rs", [128, NT], F32)
        stg = two("p3_stg", [128, NT], BF16)
        for h in range(NH):
            i2 = h % 2
            z, zk = zs[i2], "p3_zs%d" % i2
            P.dma("sp", z[:], g.projT[1536 + h * 128:1536 + (h + 1) * 128, :], reads=["projT"], writes=[zk])
            P.actf(oo[i2][:], oT[:, h, :], AF.Square, ["p3_oT"], ["p3_oo%d" % i2])
            for T in range(NTT):
                tok = slice(T * 512, (T + 1) * 512)
                pb, pk = ps[T % 8], "ps%d" % (T % 8)
                P.mm(pb[:, :], g.ones, oo[i2][:, tok], True, True, ["consts", "p3_oo%d" % i2], [pk])
                P.actf(rs[i2][:, tok], pb[:, :], AF.Sqrt, [pk], ["p3_rs%d" % i2], bias=EPS, scale=1.0 / 128)
            P.op("dve", lambda e, r=rs[i2]: e.reciprocal(r[:], r[:]), ["p3_rs%d" % i2], ["p3_rs%d" % i2])
            P.tt("dve", rs[i2][:], oT[:, h, :], rs[i2][:], ALU.mult, ["p3_oT", "p3_rs%d" % i2], ["p3_rs%d" % i2])
            P.stt("dve", stg[i2][:], rs[i2][:], g.vp[:, OB_GON:OB_GON + 1], z[:], ALU.mult, ALU.mult,
                  ["p3_rs%d" % i2, "vp", zk], ["p3_stg%d" % i2])
            P.dma("pool", g.ogT[h * 128:(h + 1) * 128, :], stg[i2][:], reads=["p3_stg%d" % i2], writes=[])
        P.barrier()


def res_bufs(sb):
    return {
        "stat": [sb("rs_stat%d" % i, [128, 4], F32) for i in range(2)],
        "tmp": [sb("rs_tmp%d" % i, [128, 512], F32) for i in range(2)],
        "junk": sb("rs_junk", [128, 512], BF16),
    }


def residual_update(g, pbs, pks, x_ap, xkey, G, Gk, bufs, n):
    P = g.P
    stat, sk = bufs["stat"][n % 2], "rs_stat%d" % (n % 2)
    P.memset("dve", stat[:], 0.0, [sk])
    for hf in range(2):
        P.actf(bufs["junk"][:], pbs[hf][:, :], AF.Square, [pks[hf], sk], ["rs_junk", sk], accum=stat[:, hf:hf + 1])
    P.tt("dve", stat[:, 2:3], stat[:, 0:1], stat[:, 1:2], ALU.add, [sk], [sk])
    P.actf(stat[:, 3:4], stat[:, 2:3], AF.Sqrt, [sk], [sk], bias=EPS, scale=1.0 / D)
    P.op("dve", lambda e: e.reciprocal(stat[:, 3:4], stat[:, 3:4]), [sk], [sk])
    for hf in range(2):
        t, tk = bufs["tmp"][hf], "rs_tmp%d" % hf
        sl = slice(hf * 512, (hf + 1) * 512)
        P.stt("dve", t[:], pbs[hf][:, :], stat[:, 3:4], G[:, sl], ALU.mult, ALU.mult, [pks[hf], sk, Gk], [tk])
        P.tt("pool", x_ap[:, sl], x_ap[:, sl], t[:], ALU.add, [xkey, tk], [xkey])


def phase_p4a(g, l):
    P = g.P
    src = g.x_in if l == 0 else g.y
    srckey = "x_dram" if l == 0 else "y_dram"
    with ExitStack() as pes:
        sb = _sbf(g, pes)
        wbf = sb("p4a_w", [128, 8, D], BF16)
        WK = load_weight_bf16(g, sb, wbf, "p4a_w", g.w_out[l], 8, D, "p4aw")
        rb = res_bufs(sb)
        xt = [sb("p4a_x%d" % i, [128, 4, D], F32) for i in range(2)]
        mixT = [sb("p4a_m%d" % i, [128, 8, 512], BF16) for i in range(2)]
        n = 0
        for T in range(g.NTT):
            tok = slice(T * 512, (T + 1) * 512)
            x, xk = xt[T % 2], "p4a_x%d" % (T % 2)
            m, mk = mixT[T % 2], "p4a_m%d" % (T % 2)
            P.dma("sp", x[:], src[tok, :].rearrange("(s p) f -> p s f", p=128), reads=[], writes=[xk])
            P.dma("sp", m[:, 0:4, :], g.ogT.rearrange("(k p) t -> p k t", p=128)[:, :, tok], reads=["ogT"], writes=[mk])
            P.dma("sp", m[:, 4:8, :], g.uT.rearrange("(k p) t -> p k t", p=128)[:, :, tok], reads=["uT"], writes=[mk])
            for s in range(4):
                b0 = (n % 2) * 2
                pbs = [g.psum[b0], g.psum[b0 + 1]]
                pks = ["ps%d" % b0, "ps%d" % (b0 + 1)]
                for hf in range(2):
                    for kc in range(8):
                        P.mm(pbs[hf][:, :], m[:, kc, s * 128:(s + 1) * 128], wbf[:, kc, hf * 512:(hf + 1) * 512], kc == 0, kc == 7,
                             [mk, WK[kc]], [pks[hf]])
                residual_update(g, pbs, pks, x[:, s, :], xk, g.Gm, "Gm", rb, n)
                n += 1
            P.dma("pool", g.y[tok, :].rearrange("(s p) f -> p s f", p=128), x[:], reads=[xk], writes=[])
        P.barrier()


def phase_p4b(g, l):
    P = g.P
    NCC = 2 * DFF // 128
    with ExitStack() as pes:
        sb = _sbf(g, pes)
        wbf = sb("p4b_w", [128, 8, 2 * DFF], BF16)
        WK = load_weight_bf16(g, sb, wbf, "p4b_w", g.w_up[l], 8, 2 * DFF, "p4bw")
        nb = nh_bufs(sb)
        xt = [sb("p4b_x%d" % i, [128, 4, D], F32) for i in range(2)]
        hT = [sb("p4b_hT%d" % i, [128, 8, 512], BF16) for i in range(2)]
        stg = [sb("p4b_stg%d" % i, [128, 512], BF16) for i in range(4)]
        cnt = {"nsub": 0, "nst": 0, "npb": 0}

        def norm_tile(T):
            tok = slice(T * 512, (T + 1) * 512)
            x, xk = xt[T % 2], "p4b_x%d" % (T % 2)
            h, hk = hT[T % 2], "p4b_hT%d" % (T % 2)
            P.dma("sp", x[:], g.y[tok, :].rearrange("(s p) f -> p s f", p=128), reads=[], writes=[xk])
            for s in range(4):
                yield from norm_to_hT(g, x[:, s, :], [xk], 2, h[:, :, s * 128:(s + 1) * 128], hk, nb, cnt["nsub"])
                cnt["nsub"] += 1

        def mm_tile(T):
            tok = slice(T * 512, (T + 1) * 512)
            h, hk = hT[T % 2], "p4b_hT%d" % (T % 2)
            for cc in range(NCC):
                pb, pk = g.psum[cnt["npb"] % 4], "ps%d" % (cnt["npb"] % 4)
                cnt["npb"] += 1
                for kc in range(8):
                    P.mm(pb[:, :], wbf[:, kc, cc * 128:(cc + 1) * 128], h[:, kc, :], kc == 0, kc == 7, [WK[kc], hk], [pk])
                st_, stk = stg[cnt["nst"] % 4], "p4b_stg%d" % (cnt["nst"] % 4)
                cnt["nst"] += 1
                P.copy("act" if cc % 2 == 0 else "dve", st_[:], pb[:, :], [pk], [stk])
                if cc < 22:
                    P.dma("pool", g.gtT[cc * 128:(cc + 1) * 128, tok], st_[:], reads=[stk], writes=[])
                else:
                    P.dma("pool", g.upT[(cc - 22) * 128:(cc - 21) * 128, tok], st_[:], reads=[stk], writes=[])
                yield

        for _ in norm_tile(0):
            pass
        for T in range(g.NTT):
            interleave(mm_tile(T), norm_tile(T + 1) if T + 1 < g.NTT else None, 0.7)
        P.barrier()


def phase_p5(g, l):
    P = g.P
    NT, NSEG = g.NT, g.NSEG
    W = SEG + 2 * PADF
    with ExitStack() as pes:
        sb = _sbf(g, pes)
        cin = [sb("p5_cin%d" % i, [128, NSEG, W], BF16) for i in range(2)]
        xL = [sb("p5_xL%d" % i, [128, NSEG, W], BF16) for i in range(2)]
        xR = [sb("p5_xR%d" % i, [128, NSEG, W], BF16) for i in range(2)]
        upr = [sb("p5_up%d" % i, [128, NT], BF16) for i in range(2)]
        gate = [sb("p5_gate%d" % i, [128, NT], BF16) for i in range(2)]
        outb = [sb("p5_out%d" % i, [128, NT], BF16) for i in range(2)]
        dgf = [sb("p5_dg%d" % i, [128, 9, 128], BF16) for i in range(2)]
        for i in range(2):
            P.memset("dve", cin[i][:], 0.0, ["p5_cin%d" % i])
            P.memset("pool", xL[i][:], 0.0, ["p5_xL%d" % i])
            P.memset("pool", xR[i][:], 0.0, ["p5_xR%d" % i])
        npb = 0
        for c in range(22):
            i2 = c % 2
            ci, ck = cin[i2], "p5_cin%d" % i2
            xl, xlk = xL[i2], "p5_xL%d" % i2
            xr, xrk = xR[i2], "p5_xR%d" % i2
            src = g.gtT[c * 128:(c + 1) * 128, :].rearrange("p (s t) -> p s t", s=NSEG)
            P.dma("sp", ci[:, :, PADF:PADF + SEG], src, reads=["gtT"], writes=[ck])
            P.dma("sp", xl[:, :, PADF:PADF + SEG], src, reads=["gtT"], writes=[xlk])
            P.dma("sp", xr[:, :, PADF:PADF + SEG], src, reads=["gtT"], writes=[xrk])
            P.dma("sp", upr[i2][:], g.upT[c * 128:(c + 1) * 128, :], reads=["upT"], writes=["p5_up%d" % i2])
            eL = slice(PADF + 63, PADF + SEG, 64)
            eR = slice(PADF, PADF + SEG, 64)
            P.tt("dve", xl[:, :, eL], xl[:, :, eL], g.ML[:, :, eL], ALU.mult, [xlk, "Mpad0"], [xlk])
            P.tt("dve", xr[:, :, eR], xr[:, :, eR], g.MR[:, :, eR], ALU.mult, [xrk, "Mpad1"], [xrk])
            _halo(g, ci, ck, PADF)
            _halo(g, xl, xlk, PADF)
            _halo(g, xr, xrk, PADF)
            dg, dgk = dgf[i2], "p5_dg%d" % i2
            P.tt("dve", dg[:], bc_mid(g.identb[:], 9), bc_last(g.vp[:, OB_CFFN + c * 9:OB_CFFN + c * 9 + 9], 128), ALU.mult,
                 ["identb", "vp"], [dgk])
            gt_, gk = gate[i2], "p5_gate%d" % i2
            for T in range(g.NTT):
                pb, pk = g.psum[npb % 8], "ps%d" % (npb % 8)
                npb += 1
                o3 = pb[:, :].rearrange("p (s t) -> p s t", s=2)
                n = 0
                for dy in range(3):
                    for dx in range(3):
                        j = dy * 3 + dx
                        sh = PADF + (dy - 1) * 64 + (dx - 1)
                        srcb, sk = ((xl, xlk) if dx == 0 else ((ci, ck) if dx == 1 else (xr, xrk)))
                        P.mm(o3, dg[:, j, :], srcb[:, 2 * T:2 * T + 2, sh:sh + SEG], n == 0, n == 8, [dgk, sk], [pk])
                        n += 1
                P.actf(gt_[:, T * 512:(T + 1) * 512], pb[:, :], AF.Silu, [pk, "vp"], [gk], bias=g.vp[:, OB_BFF + c:OB_BFF + c + 1])
            P.tt("pool" if c % 2 == 0 else "dve", outb[i2][:], gt_[:], upr[i2][:], ALU.mult, [gk, "p5_up%d" % i2], ["p5_out%d" % i2])
            P.dma("pool", g.aT[c * 128:(c + 1) * 128, :], outb[i2][:], reads=["p5_out%d" % i2], writes=[])
        P.barrier()


def phase_p6(g, l):
    P = g.P
    KC = DFF // 128
    with ExitStack() as pes:
        sb = _sbf(g, pes)
        wbf = sb("p6_w", [128, KC, D], BF16)
        WK = load_weight_bf16(g, sb, wbf, "p6_w", g.w_down[l], KC, D, "p6w")
        rb = res_bufs(sb)
        xt = [sb("p6_x%d" % i, [128, 4, D], F32) for i in range(2)]
        aTb = [sb("p6_a%d" % i, [128, KC, 512], BF16) for i in range(2)]
        n = 0
        for T in range(g.NTT):
            tok = slice(T * 512, (T + 1) * 512)
            x, xk = xt[T % 2], "p6_x%d" % (T % 2)
            m, mk = aTb[T % 2], "p6_a%d" % (T % 2)
            P.dma("sp", x[:], g.y[tok, :].rearrange("(s p) f -> p s f", p=128), reads=[], writes=[xk])
            P.dma("sp", m[:], g.aT.rearrange("(k p) t -> p k t", p=128)[:, :, tok], reads=["aT"], writes=[mk])
            for s in range(4):
                b0 = (n % 2) * 2
                pbs = [g.psum[b0], g.psum[b0 + 1]]
                pks = ["ps%d" % b0, "ps%d" % (b0 + 1)]
                for hf in range(2):
                    for kc in range(KC):
                        P.mm(pbs[hf][:, :], m[:, kc, s * 128:(s + 1) * 128], wbf[:, kc, hf * 512:(hf + 1) * 512], kc == 0, kc == KC - 1,
                             [mk, WK[kc]], [pks[hf]])
                residual_update(g, pbs, pks, x[:, s, :], xk, g.Gf, "Gf", rb, n)
                n += 1
            P.dma("pool", g.y[tok, :].rearrange("(s p) f -> p s f", p=128), x[:], reads=[xk], writes=[])
        P.barrier()


NT_FULL = 4096
DEPTH_FULL = 4
_PROG = {}


def kernel(**inputs):
    inp = {k: np.asarray(v) for k, v in inputs.items()}
    NT, DEPTH = NT_FULL, DEPTH_FULL
    shared = make_shared(inp, DEPTH)
    xs = np.asarray(inp["x_sample"], np.float32)
    xp = np.asarray(inp["x_prompt"], np.float32)
    nb_s = xs.shape[0]
    per = xp.shape[0] // (8 - nb_s)
    maps = []
    for b in range(nb_s):
        maps.append(make_core_map(inp, shared, NT, DEPTH, True, xs[b], inp["c"][b], inp["state_delta"][b]))
    zeros_s = np.zeros((DEPTH, 2, NH, 128, 128), np.float32)
    for c in range(8 - nb_s):
        real = xp[c * per:(c + 1) * per].reshape(per * SEG, D)
        x_tok = np.concatenate([real] * (NT // (per * SEG)), axis=0)
        maps.append(make_core_map(inp, shared, NT, DEPTH, False, x_tok, inp["c_ctx"], zeros_s))
    if "nc" not in _PROG:
        _PROG["nc"] = build_program(NT, DEPTH)
    res = run_bass_kernel_spmd(_PROG["nc"], maps, core_ids=list(range(8)))
    outs = res.results
    y_sample = np.stack([np.asarray(outs[b]["y"], np.float32) for b in range(nb_s)])
    y_prompt = np.concatenate([np.asarray(outs[nb_s + c]["y"], np.float32)[:per * SEG].reshape(per, SEG, D)
                               for c in range(8 - nb_s)], axis=0)
    new_state = np.concatenate([np.asarray(outs[nb_s + c]["st"], np.float32)[:per] for c in range(8 - nb_s)], axis=0)
    return (y_prompt, y_sample, new_state)
```

```python
import numpy as np
from contextlib import ExitStack
import concourse.bass as bass
import concourse.mybir as mybir
from concourse.bass_utils import run_bass_kernel_spmd

F32 = mybir.dt.float32
BF16 = mybir.dt.bfloat16
AF = mybir.ActivationFunctionType
ALU = mybir.AluOpType
AX = mybir.AxisListType

ENGS = ("pe", "act", "dve", "pool", "sp")
PSUM_KEYS = set("ps%d" % i for i in range(8))


class Prog:
    EPOCH = 20000
    NDMA = 12

    def __init__(self, nc, es):
        self.nc = nc
        self.es = es
        self.q = {e: [] for e in ENGS}
        self.cnt = {e: 0 for e in ENGS}
        self.seen = {e: {} for e in ENGS}
        self.track = {}
        self.dma_val = {}
        self.dma_rr = {e: 0 for e in ENGS}
        self.semkeys = set()
        self.nbuf = 0

    def sb(self, name, shape, dt):
        return self.es.enter_context(self.nc.sbuf_tensor(name, list(shape), dt))

    def ps(self, name, shape, dt):
        return self.es.enter_context(self.nc.psum_tensor(name, list(shape), dt))

    def _need(self, eng, tok, waits):
        if tok is None:
            return
        key, val = tok
        if key[0] == "eng" and key[1] == "pe" and eng == "pe":
            return
        s = self.seen[eng]
        if key[0] == "eng":
            for (k2, v2) in s.items():
                if k2[0] == "eng" and k2[1] == key[1] and k2[2] > key[2]:
                    return
        if s.get(key, 0) >= val:
            return
        waits[key] = max(waits.get(key, 0), val)

    def _deps(self, eng, reads, writes):
        waits = {}
        for k in reads:
            t = self.track.get(k)
            if t:
                self._need(eng, t["w"], waits)
        for k in writes:
            t = self.track.get(k)
            if t:
                self._need(eng, t["w"], waits)
                for tok in t["r"].values():
                    self._need(eng, tok, waits)
        for key, val in waits.items():
            self.seen[eng][key] = val
        return list(waits.items())

    def _commit(self, tok, reads, writes):
        for k in reads:
            t = self.track.setdefault(k, {"w": None, "r": {}})
            t["r"][tok[0]] = tok
        for k in writes:
            self.track[k] = {"w": tok, "r": {}}

    def op(self, eng, fn, reads=(), writes=()):
        writes = list(writes) + [k for k in reads if k in PSUM_KEYS]
        reads = [k for k in reads if k not in PSUM_KEYS]
        waits = self._deps(eng, reads, writes)
        self.cnt[eng] += 1
        n = self.cnt[eng]
        key = ("eng", eng, (n - 1) // self.EPOCH)
        tok = (key, (n - 1) % self.EPOCH + 1)
        self.semkeys.add(key)
        for k, _ in waits:
            self.semkeys.add(k)
        self.q[eng].append((waits, fn, key, 1))
        self._commit(tok, reads, writes)
        return tok

    def dma(self, queue, out_ap, in_ap, reads=(), writes=(), **kw):
        j = self.dma_rr[queue]
        self.dma_rr[queue] = (j + 1) % self.NDMA
        key = ("dma", queue, j)
        prev = self.dma_val.get(key, 0)
        waits = dict(self._deps(queue, reads, writes))
        if prev and self.seen[queue].get(key, 0) < prev:
            waits[key] = prev
            self.seen[queue][key] = prev
        val = prev + 16
        self.dma_val[key] = val
        self.semkeys.add(key)
        for k in waits:
            self.semkeys.add(k)
        fn = (lambda e, o=out_ap, i=in_ap, kw=kw: e.dma_start(out=o, in_=i, **kw))
        self.q[queue].append((list(waits.items()), fn, key, 16))
        tok = (key, val)
        self._commit(tok, reads, writes)
        return tok

    def wait_all(self, eng, keys):
        waits = self._deps(eng, keys, ())
        for k, _ in waits:
            self.semkeys.add(k)
        self.q[eng].append((waits, None, None, 0))

    def finish(self):
        nc = self.nc
        sems = {}
        for key in sorted(self.semkeys, key=str):
            sems[key] = self.es.enter_context(nc.semaphore("s_" + "_".join(str(x) for x in key)))
        handles = {"pe": "tensor", "act": "scalar", "dve": "vector", "pool": "gpsimd", "sp": "sync"}
        with nc.Block() as block:
            for eng in ENGS:
                lst = self.q[eng]
                if not lst:
                    continue

                def body(e, lst=lst):
                    for waits, fn, key, inc in lst:
                        for wk, wv in waits:
                            e.wait_ge(sems[wk], wv)
                        if fn is not None:
                            fn(e).then_inc(sems[key], inc)

                getattr(block, handles[eng])(body)

    def barrier(self):
        for eng in ENGS:
            waits = {}
            for other in ENGS:
                n = self.cnt[other]
                if other != eng and n > 0:
                    self._need(eng, (("eng", other, (n - 1) // self.EPOCH), (n - 1) % self.EPOCH + 1), waits)
            for key, val in self.dma_val.items():
                self._need(eng, (key, val), waits)
            for key, val in waits.items():
                self.seen[eng][key] = val
                self.semkeys.add(key)
            self.q[eng].append((list(waits.items()), None, None, 0))

    def copy(self, eng, out, in_, r, w):
        if eng == "act":
            return self.op("act", lambda e: e.copy(out, in_), r, w)
        return self.op(eng, lambda e: e.tensor_copy(out, in_), r, w)

    def tt(self, eng, out, a, b, op, r, w):
        return self.op(eng, lambda e: e.tensor_tensor(out, a, b, op), r, w)

    def ts(self, eng, out, a, s1, s2, op0, op1, r, w):
        if s2 is None:
            return self.op(eng, lambda e: e.tensor_scalar(out, a, s1, None, op0), r, w)
        return self.op(eng, lambda e: e.tensor_scalar(out, a, s1, s2, op0, op1), r, w)

    def stt(self, eng, out, a, s, b, op0, op1, r, w):
        return self.op(eng, lambda e: e.scalar_tensor_tensor(out, a, s, b, op0, op1), r, w)

    def actf(self, out, in_, func, r, w, bias=0.0, scale=1.0, accum=None):
        if accum is None:
            return self.op("act", lambda e: e.activation(out, in_, func, bias=bias, scale=scale), r, w)
        return self.op("act", lambda e: e.activation(out, in_, func, bias=bias, scale=scale, accum_out=accum), r, w)

    def mm(self, out, lhsT, rhs, start, stop, r, w):
        return self.op("pe", lambda e: e.matmul(out, lhsT, rhs, start=start, stop=stop), r, w)

    def tr(self, out, in_, ident, r, w):
        return self.op("pe", lambda e: e.transpose(out, in_, ident), r, w)

    def memset(self, eng, ap, val, w):
        return self.op(eng, lambda e: e.memset(ap, val), (), w)


def bc_last(ap, n):
    sh = list(ap.shape)
    return ap.unsqueeze(len(sh)).to_broadcast(sh + [n])


def bc_mid(ap, n):
    sh = list(ap.shape)
    return ap.unsqueeze(1).to_broadcast([sh[0], n] + sh[1:])


D = 1024
DA = 512
NH = 4
DFF = 2816
INC = 3088
SEG = 256
CH = 64
NMOD = 6
EPS = 1e-6
PADQ = 2
PADC = 15
PADF = 66
OB_BMOD, OB_GPM, OB_GQM, OB_GPF, OB_GQF, OB_CQ, OB_CCF, OB_BCF, OB_LNG, OB_LNB, OB_GON, OB_BFF, OB_CFFN = (
    0, 48, 56, 64, 72, 80, 140, 264, 268, 272, 276, 277, 299)
NV = 497


class Ctx:
    pass


def build_program(NT, DEPTH, debug=False, stop_after=None):
    assert NT % 512 == 0
    NSEG = NT // SEG
    NTT = NT // 512
    NB = NT // 128
    NCH = NT // CH
    nc = bass.Bass("TRN2", target_bir_lowering=False)

    def din(name, shape, dt=F32):
        return nc.dram_tensor(name, list(shape), dt, kind="ExternalInput").ap()

    def dout(name, shape, dt=F32):
        return nc.dram_tensor(name, list(shape), dt, kind="ExternalOutput").ap()

    def scr(name, shape, dt):
        if debug:
            return nc.dram_tensor(name, list(shape), dt, kind="ExternalOutput").ap()
        return nc.dram_tensor(name, list(shape), dt).ap()

    g = Ctx()
    g.NT, g.DEPTH, g.NSEG, g.NTT, g.NB, g.NCH = NT, DEPTH, NSEG, NTT, NB, NCH
    g.x_in = din("x", [NT, D])
    g.cvec = din("cvec", [128, 8])
    g.vecP = din("vecP", [DEPTH, 128, NV])
    g.vecB = din("vecB", [DEPTH, 16])
    g.s0 = din("s0", [DEPTH, 2, NH, 128, 128])
    g.keep_d = din("keep", [128, 1])
    g.cmask = din("cmask", [2, NT])
    g.consts_d = din("consts", [128, 512])
    g.w_mod = din("w_mod", [DEPTH, D, NMOD * D])
    g.w_in = din("w_in", [DEPTH, D, INC])
    g.w_out = din("w_out", [DEPTH, D, D])
    g.w_up = din("w_up", [DEPTH, D, 2 * DFF])
    g.w_down = din("w_down", [DEPTH, DFF, D])
    g.y = dout("y", [NT, D])
    g.st = dout("st", [NSEG, DEPTH, 2, NH, 128, 128])
    g.projT = scr("projT", [2560, NT], BF16)
    g.gb = scr("gb", [NT, 24], F32)
    g.qT = scr("qT", [DA, NT], BF16)
    g.kT = scr("kT", [DA, NT], BF16)
    g.ktok = scr("ktok", [NT, DA], BF16)
    g.vtok = scr("vtok", [NT, DA], BF16)
    g.uT = scr("uT", [DA, NT], BF16)
    g.ogT = scr("ogT", [DA, NT], BF16)
    g.gtT = scr("gtT", [DFF, NT], BF16)
    g.upT = scr("upT", [DFF, NT], BF16)
    g.aT = scr("aT", [DFF, NT], BF16)
    if debug:
        g.dbg_mod = dout("dbg_mod", [128, 48])

    with ExitStack() as es:
        P = Prog(nc, es)
        g.P = P
        g.nc = nc
        g.consts = P.sb("consts_sb", [128, 512], F32)
        g.identb = P.sb("identb", [128, 128], BF16)
        g.onesb = P.sb("onesb", [128, 128], BF16)
        g.keep = P.sb("keepf", [128, 1], F32)
        g.ML = P.sb("MLp", [128, NSEG, SEG + 2 * PADF], BF16)
        g.MR = P.sb("MRp", [128, NSEG, SEG + 2 * PADF], BF16)
        g.modP = P.sb("modP", [128, 6, 8], F32)
        g.AB = P.sb("AB", [128, 4, 8], F32)
        g.Gm = P.sb("Gm", [128, D], F32)
        g.Gf = P.sb("Gf", [128, D], F32)
        g.scv = P.sb("scv", [128, 8], F32)
        g.vp = P.sb("vecPs", [128, NV], F32)
        g.vb = P.sb("vecBs", [128, 16], F32)
        g.psum = [P.ps("psb%d" % i, [128, 512], F32) for i in range(8)]
        g.ident = g.consts[:, 0:128]
        g.ones = g.consts[:, 128:256]
        g.U = g.consts[0:64, 256:320]
        g.Us = g.consts[0:64, 320:384]
        g.L = g.consts[0:64, 384:448]
        g.Ls = g.consts[0:64, 448:512]

        phase_init(g)
        phases = [(n, globals()["phase_" + n]) for n in ("mod", "p1", "p2a", "p2b", "p3", "p4a", "p4b", "p5", "p6")
                  if ("phase_" + n) in globals()]
        done = False
        for l in range(DEPTH):
            for name, fn in phases:
                fn(g, l)
                if stop_after == name and l == DEPTH - 1:
                    done = True
                    break
            if done:
                break
        P.finish()
    return nc


def phase_init(g):
    P = g.P
    NSEG = g.NSEG
    P.dma("sp", g.consts[:], g.consts_d, writes=["consts"])
    P.dma("sp", g.keep[:], g.keep_d, writes=["keep"])
    P.copy("dve", g.identb[:], g.ident, ["consts"], ["identb"])
    P.copy("dve", g.onesb[:], g.ones, ["consts"], ["onesb"])
    P.dma("sp", g.scv[:], g.cvec, writes=["scv"])
    P.actf(g.scv[:], g.scv[:], AF.Silu, ["scv"], ["scv"])
    with ExitStack() as pes:
        mtmp = _sbf(g, pes)("mtmp", [128, 2, g.NT], F32)
        P.dma("sp", mtmp[:, 0, :], g.cmask[0, :].partition_broadcast(128), writes=["mtmp0"])
        P.dma("sp", mtmp[:, 1, :], g.cmask[1, :].partition_broadcast(128), writes=["mtmp1"])
        for i, M in enumerate((g.ML, g.MR)):
            key = "Mpad%d" % i
            P.memset("dve", M[:], 0.0, [key])
            P.copy("dve", M[:, :, PADF:PADF + SEG], mtmp[:, i, :].rearrange("p (s t) -> p s t", s=NSEG), ["mtmp%d" % i], [key])
            if NSEG > 1:
                P.copy("dve", M[:, 1:, 0:PADF], M[:, :-1, SEG:SEG + PADF], [key], [key])
                P.copy("dve", M[:, :-1, PADF + SEG:], M[:, 1:, PADF:2 * PADF], [key], [key])
        P.barrier()


_UNIQ = [0]


def _sbf(g, pes):
    def f(name, shape, dt):
        _UNIQ[0] += 1
        return pes.enter_context(g.nc.sbuf_tensor("%s_u%d" % (name, _UNIQ[0]), list(shape), dt))
    return f


def phase_mod(g, l):
    P = g.P
    with ExitStack() as pes:
        sb = _sbf(g, pes)
        P.dma("sp", g.vp[:], g.vecP[l], writes=["vp"])
        P.dma("sp", g.vb[:], g.vecB[l].partition_broadcast(128), writes=["vb"])
        P.actf(g.vb[:, 0:8], g.vb[:, 0:8], AF.Exp, ["vb"], ["vb"])
        P.ts("dve", g.vb[:, 0:8], g.vb[:, 0:8], -1.0, None, ALU.mult, None, ["vb"], ["vb"])
        wst = [sb("wmst%d" % i, [128, 8, 1024], BF16) for i in range(3)]
        scb = sb("scvb", [128, 8], BF16)
        P.copy("dve", scb[:], g.scv[:], ["scv"], ["scvb"])
        ps = g.psum[0]
        for j in range(6):
            w = wst[j % 3]
            wkey = "wmst%d" % (j % 3)
            P.dma("pool", w[:], g.w_mod[l, :, j * 1024:(j + 1) * 1024].rearrange("(k p) c -> p k c", p=128), writes=[wkey])
            for cc in range(8):
                for kc in range(8):
                    P.mm(ps[:, j * 8 + cc:j * 8 + cc + 1], w[:, kc, cc * 128:(cc + 1) * 128], scb[:, kc:kc + 1],
                         kc == 0, kc == 7, [wkey, "scvb"], ["ps0"])
        mp = g.modP[:].rearrange("p j k -> p (j k)")
        P.tt("dve", mp, ps[:, 0:48], g.vp[:, OB_BMOD:OB_BMOD + 48], ALU.add, ["ps0", "vp"], ["modP"])
        if hasattr(g, "dbg_mod") and l == 0:
            P.dma("pool", g.dbg_mod, mp, reads=["modP"], writes=["dbg_mod"])
        P.stt("dve", g.AB[:, 0, :], g.modP[:, 1, :], 1.0, g.vp[:, OB_GPM:OB_GPM + 8], ALU.add, ALU.mult, ["modP", "vp"], ["AB"])
        P.copy("dve", g.AB[:, 1, :], g.modP[:, 0, :], ["modP"], ["AB"])
        P.stt("dve", g.AB[:, 2, :], g.modP[:, 4, :], 1.0, g.vp[:, OB_GPF:OB_GPF + 8], ALU.add, ALU.mult, ["modP", "vp"], ["AB"])
        P.copy("dve", g.AB[:, 3, :], g.modP[:, 3, :], ["modP"], ["AB"])
        gcol = sb("gcol", [128, 2, 8], F32)
        P.tt("dve", gcol[:, 0, :], g.modP[:, 2, :], g.vp[:, OB_GQM:OB_GQM + 8], ALU.mult, ["modP", "vp"], ["gcol"])
        P.tt("dve", gcol[:, 1, :], g.modP[:, 5, :], g.vp[:, OB_GQF:OB_GQF + 8], ALU.mult, ["modP", "vp"], ["gcol"])
        dg = [sb("dgm%d" % i, [128, 128], F32) for i in range(2)]
        n = 0
        for gi, G in enumerate((g.Gm, g.Gf)):
            gk = "Gm" if gi == 0 else "Gf"
            for kc in range(8):
                d = dg[n % 2]
                dk = "dgm%d" % (n % 2)
                pb = g.psum[1 + n % 2]
                pk = "ps%d" % (1 + n % 2)
                P.ts("dve", d[:], g.ident, gcol[:, gi, kc:kc + 1], None, ALU.mult, None, ["consts", "gcol"], [dk])
                P.mm(pb[:, 0:128], g.ones, d[:], True, True, ["consts", dk], [pk])
                P.copy("act", G[:, kc * 128:(kc + 1) * 128], pb[:, 0:128], [pk], [gk])
                n += 1
        P.barrier()


def load_weight_bf16(g, sb, dst, dst_key, src2d, KC, N, name):
    P = g.P
    for kc in range(KC):
        P.dma("pool", dst[:, kc, :], src2d[kc * 128:(kc + 1) * 128, :], writes=["%s_%d" % (dst_key, kc)])
    return ["%s_%d" % (dst_key, kc) for kc in range(KC)]


def norm_to_hT(g, x_ap, xkeys, ab_i, hT_out, hkey, bufs, n):
    P = g.P
    stat, skey = bufs["stat"][n % 2], "nh_stat%d" % (n % 2)
    xn, xnkey = bufs["xn"][n % 2], "nh_xn%d" % (n % 2)
    tmp, tkey = bufs["tmp"][n % 2], "nh_tmp%d" % (n % 2)
    junk = bufs["junk"]
    pb = g.psum[6 + n % 2]
    pk = "ps%d" % (6 + n % 2)
    P.memset("dve", stat[:], 0.0, [skey])
    P.actf(junk[:], x_ap, AF.Square, xkeys + [skey], ["nh_junk", skey], accum=stat[:, 0:1])
    yield
    P.actf(stat[:, 1:2], stat[:, 0:1], AF.Sqrt, [skey], [skey], bias=EPS, scale=1.0 / D)
    P.op("dve", lambda e: e.reciprocal(stat[:, 2:3], stat[:, 1:2]), [skey], [skey])
    yield
    P.actf(xn[:], x_ap, AF.Identity, xkeys + [skey], [xnkey], scale=stat[:, 2:3])
    yield
    pv = pb[:].bitcast(BF16)
    for kc in range(8):
        P.tr(pv[:, kc * 128:(kc + 1) * 128], xn[:, kc * 128:(kc + 1) * 128], g.identb[:], [xnkey, "identb"], [pk])
        if kc == 3:
            yield
    yield
    pv3 = pv.rearrange("p (k t) -> p k t", k=8)
    P.tt("dve", tmp[:], pv3, bc_last(g.AB[:, ab_i, :], 128), ALU.mult, [pk, "AB"], [tkey])
    yield
    P.tt("dve", hT_out, tmp[:], bc_last(g.AB[:, ab_i + 1, :], 128), ALU.add, [tkey, "AB"], [hkey])
    yield


def interleave(main, side, ratio):
    acc = 0.0
    side_done = side is None
    for _ in main:
        acc += ratio
        while acc >= 1.0 and not side_done:
            acc -= 1.0
            try:
                next(side)
            except StopIteration:
                side_done = True
    while not side_done:
        try:
            next(side)
        except StopIteration:
            side_done = True


def nh_bufs(sb):
    return {
        "stat": [sb("nh_stat%d" % i, [128, 4], F32) for i in range(2)],
        "xn": [sb("nh_xn%d" % i, [128, D], BF16) for i in range(2)],
        "tmp": [sb("nh_tmp%d" % i, [128, 8, 128], F32) for i in range(2)],
        "junk": sb("nh_junk", [128, D], BF16),
    }


def phase_p1(g, l):
    P = g.P
    src = g.x_in if l == 0 else g.y
    srckey = "x_dram" if l == 0 else "y_dram"
    with ExitStack() as pes:
        sb = _sbf(g, pes)
        wbf = sb("p1_w", [128, 8, INC], BF16)
        WK = load_weight_bf16(g, sb, wbf, "p1_w", g.w_in[l], 8, INC, "p1w")
        nb = nh_bufs(sb)
        xt = [sb("p1_x%d" % i, [128, 4, D], F32) for i in range(2)]
        hT = [sb("p1_hT%d" % i, [128, 8, 512], BF16) for i in range(2)]
        stg = [sb("p1_stg%d" % i, [128, 512], BF16) for i in range(4)]
        sig = [sb("p1_sig%d" % i, [128, 512], F32) for i in range(2)]
        gbt = [sb("p1_gbt%d" % i, [128, 24], F32) for i in range(2)]
        t8 = [sb("p1_t8%d" % i, [128, 8], F32) for i in range(2)]
        cnt = {"nsub": 0, "nst": 0}

        def norm_tile(T):
            x, xk = xt[T % 2], "p1_x%d" % (T % 2)
            h, hk = hT[T % 2], "p1_hT%d" % (T % 2)
            P.dma("sp", x[:], src[T * 512:(T + 1) * 512, :].rearrange("(s p) f -> p s f", p=128), reads=[], writes=[xk])
            for s in range(4):
                yield from norm_to_hT(g, x[:, s, :], [xk], 0, h[:, :, s * 128:(s + 1) * 128], hk, nb, cnt["nsub"])
                cnt["nsub"] += 1

        def mm_tile(T):
            h, hk = hT[T % 2], "p1_hT%d" % (T % 2)
            tok = slice(T * 512, (T + 1) * 512)
            npb = 0
            seq = [("c", cc) for cc in range(16)]
            for i in range(4):
                seq += [("gb", i), ("ga", i)]
            for kind, i in seq:
                col0 = i * 128 if kind == "c" else (2576 + 128 * i if kind == "gb" else 2064 + 128 * i)
                pb = g.psum[npb % 4]
                pk = "ps%d" % (npb % 4)
                npb += 1
                for kc in range(8):
                    P.mm(pb[:, :], wbf[:, kc, col0:col0 + 128], h[:, kc, :], kc == 0, kc == 7, [WK[kc], hk], [pk])
                if kind == "gb":
                    sg, sgk = sig[i % 2], "p1_sig%d" % (i % 2)
                    P.actf(sg[:], pb[:, :], AF.Sigmoid, [pk], [sgk])
                    yield
                    continue
                st_, stk = stg[cnt["nst"] % 4], "p1_stg%d" % (cnt["nst"] % 4)
                cnt["nst"] += 1
                if kind == "ga":
                    P.tt("dve", st_[:], pb[:, :], sig[i % 2][:], ALU.mult, [pk, "p1_sig%d" % (i % 2)], [stk])
                    row0 = 2048 + i * 128
                elif i >= 12:
                    P.actf(st_[:], pb[:, :], AF.Silu, [pk], [stk])
                    row0 = i * 128
                else:
                    P.copy("act" if i % 2 == 0 else "dve", st_[:], pb[:, :], [pk], [stk])
                    row0 = i * 128
                P.dma("pool", g.projT[row0:row0 + 128, tok], st_[:], reads=[stk], writes=[])
                yield
            for s in range(4):
                pb = g.psum[4 + s % 2]
                pk = "ps%d" % (4 + s % 2)
                for kc in range(8):
                    P.mm(pb[:, 0:16], h[:, kc, s * 128:(s + 1) * 128], wbf[:, kc, 2048:2064], kc == 0, kc == 7, [WK[kc], hk], [pk])
                gt_, gk = gbt[s % 2], "p1_gbt%d" % (s % 2)
                t, tk = t8[s % 2], "p1_t8%d" % (s % 2)
                P.tt("dve", t[:], pb[:, 0:8], g.vb[:, 8:16], ALU.add, [pk, "vb"], [tk])
                P.actf(t[:], t[:], AF.Exp, [tk], [tk])
                P.actf(t[:], t[:], AF.Ln, [tk], [tk], bias=1.0)
                P.tt("dve", gt_[:, 0:8], t[:], g.vb[:, 0:8], ALU.mult, [tk, "vb"], [gk])
                P.actf(gt_[:, 8:16], pb[:, 8:16], AF.Sigmoid, [pk], [gk])
                P.ts("dve", gt_[:, 16:24], gt_[:, 8:16], -1.0, None, ALU.mult, None, [gk], [gk])
                P.dma("pool", g.gb[T * 512 + s * 128:T * 512 + (s + 1) * 128, :], gt_[:], reads=[gk], writes=[])
                yield

        for _ in norm_tile(0):
            pass
        for T in range(g.NTT):
            interleave(mm_tile(T), norm_tile(T + 1) if T + 1 < g.NTT else None, 1.2)
        P.barrier()


def _pp(v):
    return np.ascontiguousarray(np.asarray(v, np.float32).reshape(-1, 128).T)


def make_consts():
    c = np.zeros((128, 512), np.float32)
    c[:, 0:128] = np.eye(128, dtype=np.float32)
    c[:, 128:256] = 1.0
    i = np.arange(64)
    c[0:64, 256:320] = (i[:, None] <= i[None, :])
    c[0:64, 320:384] = (i[:, None] < i[None, :])
    c[0:64, 384:448] = (i[:, None] >= i[None, :])
    c[0:64, 448:512] = (i[:, None] > i[None, :])
    return c


def make_shared(inp, DEPTH):
    f = lambda a: np.asarray(a, np.float32)
    packs = []
    for grid in (True, False):
        vp = np.zeros((DEPTH, 128, NV), np.float32)
        for l in range(DEPTH):
            vp[l, :, OB_BMOD:OB_BMOD + 48] = _pp(f(inp["b_mod"])[l])
            vp[l, :, OB_GPM:OB_GPM + 8] = _pp(f(inp["g_pre_mix"])[l])
            vp[l, :, OB_GQM:OB_GQM + 8] = _pp(f(inp["g_post_mix"])[l])
            vp[l, :, OB_GPF:OB_GPF + 8] = _pp(f(inp["g_pre_ffn"])[l])
            vp[l, :, OB_GQF:OB_GQF + 8] = _pp(f(inp["g_post_ffn"])[l])
            vp[l, :, OB_CQ:OB_CQ + 60] = f(inp["conv_qkv"])[l].reshape(5, 12, 128).transpose(2, 1, 0).reshape(128, 60)
            vp[l, :, OB_CCF:OB_CCF + 124] = f(inp["conv_cf"])[l].reshape(31, 4, 128).transpose(2, 1, 0).reshape(128, 124)
            vp[l, :, OB_BCF:OB_BCF + 4] = _pp(f(inp["b_conv_cf"])[l])
            vp[l, :, OB_LNG:OB_LNG + 4] = _pp(f(inp["ln_cf_g"])[l])
            vp[l, :, OB_LNB:OB_LNB + 4] = _pp(f(inp["ln_cf_b"])[l])
            vp[l, :, OB_GON:OB_GON + 1] = f(inp["g_onorm"])[l].reshape(128, 1)
            vp[l, :, OB_BFF:OB_BFF + 22] = _pp(f(inp["b_conv_ffn"])[l])
            cf = f(inp["conv_ffn"])[l].reshape(9, DFF)
            if not grid:
                sel = np.zeros_like(cf)
                sel[3:6] = cf[3:6]
                cf = sel
            vp[l, :, OB_CFFN:OB_CFFN + 198] = cf.reshape(9, 22, 128).transpose(2, 1, 0).reshape(128, 198)
        packs.append(vp)
    vb = np.concatenate([f(inp["a_log"]).reshape(DEPTH, 8), f(inp["dt_bias"]).reshape(DEPTH, 8)], axis=1)
    return packs[0], packs[1], np.ascontiguousarray(vb)


def make_core_map(inp, shared, NT, DEPTH, grid, x_tok, cvec, s0):
    vp_grid, vp_seq, vb = shared
    t = np.arange(NT)
    if grid:
        cm = np.stack([(t % 64 != 63), (t % 64 != 0)]).astype(np.float32)
        keep = np.ones((128, 1), np.float32)
    else:
        cm = np.ones((2, NT), np.float32)
        keep = np.zeros((128, 1), np.float32)
    m = {
        "x": np.ascontiguousarray(x_tok, np.float32),
        "cvec": _pp(cvec),
        "vecP": vp_grid if grid else vp_seq,
        "vecB": vb,
        "s0": np.ascontiguousarray(s0, np.float32),
        "keep": keep,
        "cmask": cm,
        "consts": make_consts(),
    }
    for k in ("w_mod", "w_in", "w_out", "w_up", "w_down"):
        m[k] = np.ascontiguousarray(np.asarray(inp[k], np.float32)[:DEPTH])
    return m


def _halo(g, cin, key, pad):
    P = g.P
    if g.NSEG > 1:
        P.ts("dve", cin[:, 1:, 0:pad], cin[:, :-1, SEG:SEG + pad], g.keep[:, 0:1], None, ALU.mult, None, [key, "keep"], [key])
        P.ts("dve", cin[:, :-1, pad + SEG:pad + SEG + pad], cin[:, 1:, pad:2 * pad], g.keep[:, 0:1], None, ALU.mult, None, [key, "keep"], [key])


def phase_p2a(g, l):
    P = g.P
    NT, NSEG, NTT, NB = g.NT, g.NSEG, g.NTT, g.NB
    with ExitStack() as pes:
        sb = _sbf(g, pes)
        cin = [sb("p2_cin%d" % i, [128, NSEG, SEG + 2 * PADQ], BF16) for i in range(2)]
        acc = [sb("p2_acc%d" % i, [128, NT], F32) for i in range(2)]
        sq = sb("p2_sq", [128, NT], F32)
        rs = [sb("p2_rs%d" % i, [128, 512], F32) for i in range(2)]
        obf = [sb("p2_obf%d" % i, [128, NT], BF16) for i in range(2)]
        tst = [sb("p2_tst%d" % i, [128, 8, 128], BF16) for i in range(2)]
        dgq = [sb("p2_dg%d" % i, [128, 5, 128], BF16) for i in range(2)]
        for i in range(2):
            P.memset("dve", cin[i][:], 0.0, ["p2_cin%d" % i])
        cnt = {"ntr": 0}

        def stage1(cc):
            ci, ck = cin[cc % 2], "p2_cin%d" % (cc % 2)
            a, ak = acc[cc % 2], "p2_acc%d" % (cc % 2)
            ob, obk = obf[cc % 2], "p2_obf%d" % (cc % 2)
            P.dma("sp", ci[:, :, PADQ:PADQ + SEG], g.projT[cc * 128:(cc + 1) * 128, :].rearrange("p (s t) -> p s t", s=NSEG),
                  reads=["projT"], writes=[ck])
            _halo(g, ci, ck, PADQ)
            dg, dgk = dgq[cc % 2], "p2_dg%d" % (cc % 2)
            P.tt("pool", dg[:], bc_mid(g.identb[:], 5), bc_last(g.vp[:, OB_CQ + cc * 5:OB_CQ + cc * 5 + 5], 128), ALU.mult,
                 ["identb", "vp"], [dgk])
            yield
            for T in range(NTT):
                pb, pk = g.psum[4 + T % 4], "ps%d" % (4 + T % 4)
                o3 = pb[:, :].rearrange("p (s t) -> p s t", s=2)
                for j in range(5):
                    P.mm(o3, dg[:, j, :], ci[:, 2 * T:2 * T + 2, j:j + SEG], j == 0, j == 4, [dgk, ck], [pk])
                tokc = slice(T * 512, (T + 1) * 512)
                if cc >= 8:
                    P.actf(ob[:, tokc], pb[:, :], AF.Silu, [pk], [obk])
                else:
                    P.actf(a[:, tokc], pb[:, :], AF.Silu, [pk], [ak])
                yield

        def stage2(cc):
            a, ak = acc[cc % 2], "p2_acc%d" % (cc % 2)
            ob, obk = obf[cc % 2], "p2_obf%d" % (cc % 2)
            head = cc % 4
            if cc < 8:
                P.tt("dve", sq[:], a[:], a[:], ALU.mult, [ak], ["p2_sq"])
                yield
                for T in range(NTT):
                    tok = slice(T * 512, (T + 1) * 512)
                    pb, pk = g.psum[T % 2], "ps%d" % (T % 2)
                    r, rk = rs[T % 2], "p2_rs%d" % (T % 2)
                    P.mm(pb[:, :], g.ones, sq[:, tok], True, True, ["consts", "p2_sq"], [pk])
                    P.actf(r[:], pb[:, :], AF.Sqrt, [pk], [rk], bias=EPS)
                    P.op("dve", lambda e, r=r: e.reciprocal(r[:], r[:]), [rk], [rk])
                    if cc < 4:
                        P.stt("dve", ob[:, tok], a[:, tok], float(128 ** -0.5), r[:], ALU.mult, ALU.mult, [ak, rk], [obk])
                    else:
                        P.tt("dve", ob[:, tok], a[:, tok], r[:], ALU.mult, [ak, rk], [obk])
                    yield
                dst = g.qT if cc < 4 else g.kT
                P.dma("pool", dst[head * 128:(head + 1) * 128, :], ob[:], reads=[obk], writes=[])
            if cc >= 4:
                dstt = g.ktok if cc < 8 else g.vtok
                dv = dstt.rearrange("(b p) f -> p b f", p=128)
                for b0 in range(0, NB, 8):
                    nbk = min(8, NB - b0)
                    ntr = cnt["ntr"]
                    cnt["ntr"] += 1
                    pb, pk = g.psum[2 + ntr % 2], "ps%d" % (2 + ntr % 2)
                    ts_, tk = tst[ntr % 2], "p2_tst%d" % (ntr % 2)
                    pv = pb[:].bitcast(BF16)
                    for b in range(nbk):
                        P.tr(pv[:, b * 128:(b + 1) * 128], ob[:, (b0 + b) * 128:(b0 + b + 1) * 128], g.identb[:], [obk, "identb"], [pk])
                    P.copy("act", ts_[:, 0:nbk, :], pv[:, 0:nbk * 128].rearrange("p (b f) -> p b f", b=nbk), [pk], [tk])
                    P.dma("pool", dv[:, b0:b0 + nbk, head * 128:(head + 1) * 128], ts_[:, 0:nbk, :], reads=[tk], writes=[])
                    yield

        for _ in stage1(0):
            pass
        for cc in range(12):
            interleave(stage2(cc), stage1(cc + 1) if cc + 1 < 12 else None, 1.0)
        P.barrier()


def phase_p2b(g, l):
    P = g.P
    NT, NSEG, NTT = g.NT, g.NSEG, g.NTT
    with ExitStack() as pes:
        sb = _sbf(g, pes)
        cin = [sb("p2b_cin%d" % i, [128, NSEG, SEG + 2 * PADC], BF16) for i in range(4)]
        dgc = [sb("p2b_dg%d" % i, [128, 31, 128], BF16) for i in range(4)]
        ucv = [sb("p2b_ucv%d" % i, [128, 4, 512], F32) for i in range(2)]
        sq4 = [sb("p2b_sq%d" % i, [128, 4, 512], F32) for i in range(2)]
        mean = [sb("p2b_mean%d" % i, [128, 512], F32) for i in range(2)]
        msq = [sb("p2b_msq%d" % i, [128, 512], F32) for i in range(2)]
        rstd = [sb("p2b_rstd%d" % i, [128, 512], F32) for i in range(2)]
        t1 = [sb("p2b_t1%d" % i, [128, 512], F32) for i in range(2)]
        stg = [sb("p2b_stg%d" % i, [128, 512], BF16) for i in range(2)]
        for c in range(4):
            ci, ck = cin[c], "p2b_cin%d" % c
            P.memset("dve" if c % 2 == 0 else "pool", ci[:], 0.0, [ck])
            P.dma("sp", ci[:, :, PADC:PADC + SEG], g.projT[2048 + c * 128:2048 + (c + 1) * 128, :].rearrange("p (s t) -> p s t", s=NSEG),
                  reads=["projT"], writes=[ck])
            _halo(g, ci, ck, PADC)
            P.tt("pool", dgc[c][:], bc_mid(g.identb[:], 31), bc_last(g.vp[:, OB_CCF + c * 31:OB_CCF + c * 31 + 31], 128), ALU.mult,
                 ["identb", "vp"], ["p2b_dg%d" % c])
        cnt = {"n": 0}

        def conv_tile(T):
            i2 = T % 2
            for c in range(4):
                pb, pk = g.psum[4 + c], "ps%d" % (4 + c)
                o3 = pb[:, :].rearrange("p (s t) -> p s t", s=2)
                for j in range(31):
                    P.mm(o3, dgc[c][:, j, :], cin[c][:, 2 * T:2 * T + 2, j:j + SEG], j == 0, j == 30, ["p2b_dg%d" % c, "p2b_cin%d" % c], [pk])
                    if j % 8 == 7:
                        yield
                P.actf(ucv[i2][:, c, :], pb[:, :], AF.Identity, [pk, "vp"], ["p2b_ucv%d" % i2], bias=g.vp[:, OB_BCF + c:OB_BCF + c + 1])
                yield

        def ln_tile(T):
            tok = slice(T * 512, (T + 1) * 512)
            i2 = T % 2
            uk = "p2b_ucv%d" % i2
            P.tt("pool", sq4[i2][:], ucv[i2][:], ucv[i2][:], ALU.mult, [uk], ["p2b_sq%d" % i2])
            p1, p1k = g.psum[0 + 2 * i2], "ps%d" % (0 + 2 * i2)
            p2, p2k = g.psum[1 + 2 * i2], "ps%d" % (1 + 2 * i2)
            for c in range(4):
                P.mm(p1[:, :], g.ones, ucv[i2][:, c, :], c == 0, c == 3, ["consts", uk], [p1k])
            yield
            for c in range(4):
                P.mm(p2[:, :], g.ones, sq4[i2][:, c, :], c == 0, c == 3, ["consts", "p2b_sq%d" % i2], [p2k])
            yield
            mk, qk, rk = "p2b_mean%d" % i2, "p2b_msq%d" % i2, "p2b_rstd%d" % i2
            P.actf(mean[i2][:], p1[:, :], AF.Identity, [p1k], [mk], scale=1.0 / DA)
            P.tt("dve", msq[i2][:], mean[i2][:], mean[i2][:], ALU.mult, [mk], [qk])
            P.stt("dve", rstd[i2][:], p2[:, :], 1.0 / DA, msq[i2][:], ALU.mult, ALU.subtract, [p2k, qk], [rk])
            yield
            P.actf(rstd[i2][:], rstd[i2][:], AF.Sqrt, [rk], [rk], bias=EPS)
            P.op("dve", lambda e, r=rstd[i2]: e.reciprocal(r[:], r[:]), [rk], [rk])
            yield
            for c in range(4):
                n = cnt["n"]
                cnt["n"] += 1
                t, tk = t1[n % 2], "p2b_t1%d" % (n % 2)
                s_, sk = stg[n % 2], "p2b_stg%d" % (n % 2)
                P.tt("dve", t[:], ucv[i2][:, c, :], mean[i2][:], ALU.subtract, [uk, mk], [tk])
                P.tt("dve", t[:], t[:], rstd[i2][:], ALU.mult, [tk, rk], [tk])
                P.actf(s_[:], t[:], AF.Silu, [tk, "vp"], [sk], bias=g.vp[:, OB_LNB + c:OB_LNB + c + 1],
                       scale=g.vp[:, OB_LNG + c:OB_LNG + c + 1])
                P.dma("pool", g.uT[c * 128:(c + 1) * 128, tok], s_[:], reads=[sk], writes=[])
                yield

        for _ in conv_tile(0):
            pass
        for T in range(NTT):
            interleave(conv_tile(T + 1) if T + 1 < NTT else iter(()), ln_tile(T), 0.6)
        P.barrier()


def phase_p3(g, l):
    P = g.P
    NT, NTT, NCH, NSEG = g.NT, g.NTT, g.NCH, g.NSEG
    CPS = SEG // CH
    ps = g.psum
    id64 = g.consts[0:64, 0:64]
    ones64 = g.consts[0:64, 128:256]
    idb64 = g.identb[0:64, 0:64]
    with ExitStack() as pes:
        sb = _sbf(g, pes)
        oT = sb("p3_oT", [128, NH, NT], F32)
        pes2 = ExitStack()
        sb_outer = sb
        sb = _sbf(g, pes2)
        S = sb("p3_S", [128, NH, 128], F32)
        Sb = sb("p3_Sb", [128, NH, 128], BF16)
        t2 = sb("p3_t2", [128, NH, 128], F32)
        NSET = 3
        kTb = [sb("p3_kTb%d" % i, [128, NH, SEG], BF16) for i in range(NSET)]
        qTb = [sb("p3_qTb%d" % i, [128, NH, SEG], BF16) for i in range(NSET)]
        ktb = [sb("p3_ktb%d" % i, [64, CPS, 512], BF16) for i in range(NSET)]
        vtb = [sb("p3_vtb%d" % i, [64, CPS, 512], BF16) for i in range(NSET)]
        gbb = [sb("p3_gbb%d" % i, [64, CPS, 24], F32) for i in range(NSET)]

        def two(name, shape, dt):
            return [sb("%s%d" % (name, i), shape, dt) for i in range(2)]
        smK = two("p3_sm", [128, CPS, 24], F32)
        YK = two("p3_Y", [64, CPS, NH, 64], BF16)
        qkdK = two("p3_qkd", [64, CPS, NH, 64], BF16)
        qgTK = two("p3_qgT", [128, CPS, NH, 64], BF16)
        kdecK = two("p3_kdec", [64, CPS, NH, 128], BF16)
        Pq = [sb("p3_P%d" % i, [64, CPS, NH, 64], BF16) for i in range(2)]
        PTq = [sb("p3_PT%d" % i, [64, CPS, NH, 64], BF16) for i in range(2)]
        Gtri4 = sb("p3_Gtri4", [64, CPS, NH, 64], F32)
        Dg4 = sb("p3_Dg4", [64, CPS, NH, 64], F32)
        dec = two("p3_dec", [64, NH, 64], F32)
        decm = two("p3_decm", [64, NH, 64], F32)
        decs = two("p3_decs", [64, NH, 64], F32)
        tmpk = two("p3_tmpk", [64, NH, 64], F32)
        t1 = two("p3_t1", [64, NH, 128], F32)
        nr = two("p3_nr", [64, NH, 128], BF16)
        vnew = two("p3_vnew", [64, NH, 128], BF16)

        def v4(ap):
            return ap.rearrange("p (h c) -> p h c", h=NH)

        seq = [(d, B) for d in range(2) for B in (range(NSEG) if d == 0 else range(NSEG - 1, -1, -1))]
        cnt = {"tr": 0}

        def load_block(i):
            d, B = seq[i]
            st_ = i % NSET
            tok = slice(B * SEG, (B + 1) * SEG)
            P.dma("sp", kTb[st_][:], g.kT.rearrange("(h p) t -> p h t", p=128)[:, :, tok], reads=["kT"], writes=["p3_kTb%d" % st_])
            P.dma("sp", qTb[st_][:], g.qT.rearrange("(h p) t -> p h t", p=128)[:, :, tok], reads=["qT"], writes=["p3_qTb%d" % st_])
            P.dma("sp", ktb[st_][:], g.ktok[tok, :].rearrange("(c p) f -> p c f", p=64), reads=["ktok"], writes=["p3_ktb%d" % st_])
            P.dma("sp", vtb[st_][:], g.vtok[tok, :].rearrange("(c p) f -> p c f", p=64), reads=["vtok"], writes=["p3_vtb%d" % st_])
            P.dma("sp", gbb[st_][:], g.gb[tok, :].rearrange("(c p) j -> p c j", p=64), reads=["gb"], writes=["p3_gbb%d" % st_])

        def pre_gen(i):
            d, B = seq[i]
            st_ = i % NSET
            ks = i % 2
            TRI = g.U if d == 0 else g.L
            STRICT = g.Ls if d == 0 else g.Us
            MI = g.U if d == 0 else g.L
            MS = g.Us if d == 0 else g.Ls
            kk, qk_, ktk, vtk, gbk = ["p3_%s%d" % (n, st_) for n in ("kTb", "qTb", "ktb", "vtb", "gbb")]
            if i + 1 < len(seq):
                load_block(i + 1)
            KK = lambda n, ci: "p3k_%s_%d_%d" % (n, ks, ci)
            backs = []
            for ci in range(CPS):
                P.tt("pool", Gtri4[:, ci], bc_mid(TRI, NH), bc_last(gbb[st_][:, ci, d * 4:d * 4 + 4], 64), ALU.mult, ["consts", gbk],
                     ["p3_Gtri4_%d" % ci])

            def emit_back(item):
                ci_, AT_, A_, atk_, ak_ = item
                atr = ps[0][0:64, 264:392].bitcast(BF16)
                for h in range(NH):
                    P.tr(atr[:, h * 64:(h + 1) * 64], AT_[:, h, :], idb64, [atk_, "identb"], ["ps0"])
                P.copy("act", A_, v4(atr), ["ps0"], [ak_])
                P.tt("pool", YK[ks][:, ci_], AT_, bc_mid(id64, NH), ALU.add, [atk_, "consts"], [KK("Y", ci_)])

            for ci in range(CPS):
                par = cnt["tr"] % 2
                cnt["tr"] += 1
                TK = lambda n: "p3_%s%d" % (n, par)
                gv = gbb[st_][:, ci, d * 4:d * 4 + 4]
                nbv = gbb[st_][:, ci, 16 + d * 4:16 + d * 4 + 4]
                kTc = kTb[st_][:, :, ci * 64:(ci + 1) * 64]
                qTc = qTb[st_][:, :, ci * 64:(ci + 1) * 64]
                smp = smK[ks][:, ci, :]
                smk = KK("sm", ci)
                egc, etot, totsb, tmg, edec = smp[0:64, 0:4], smp[:, 4:8], smp[:, 8:12], smp[0:64, 12:16], smp[0:64, 16:20]
                P.mm(ps[0][0:64, 0:4], TRI, gv, True, True, ["consts", gbk], ["ps0"])
                P.mm(ps[0][:, 4:8], ones64, gv, True, True, ["consts", gbk], ["ps0"])
                P.mm(ps[0][0:64, 8:264], STRICT, Gtri4[:, ci].rearrange("p h c -> p (h c)"), True, True, ["consts", "p3_Gtri4_%d" % ci], ["ps0"])
                P.actf(egc, ps[0][0:64, 0:4], AF.Exp, ["ps0"], [smk])
                P.actf(etot, ps[0][:, 4:8], AF.Exp, ["ps0"], [smk])
                P.copy("act", totsb, ps[0][:, 4:8], ["ps0"], [smk])
                P.actf(dec[par][:], v4(ps[0][0:64, 8:264]), AF.Exp, ["ps0"], [TK("dec")])
                P.tt("dve", tmg, smp[0:64, 8:12], ps[0][0:64, 0:4], ALU.subtract, [smk, "ps0"], [smk])
                P.actf(edec, tmg, AF.Exp, [smk], [smk])
                P.tt("dve", decm[par][:], dec[par][:], bc_mid(MI, NH), ALU.mult, [TK("dec"), "consts"], [TK("decm")])
                P.tt("pool", decs[par][:], dec[par][:], bc_mid(MS, NH), ALU.mult, [TK("dec"), "consts"], [TK("decs")])
                yield
                for h in range(NH):
                    P.mm(ps[1][0:64, h * 64:(h + 1) * 64], kTc[:, h, :], kTc[:, h, :], True, True, [kk], ["ps1"])
                for h in range(NH):
                    P.mm(ps[1][0:64, 256 + h * 64:256 + (h + 1) * 64], kTc[:, h, :], qTc[:, h, :], True, True, [kk, qk_], ["ps1"])
                AT = PTq[0][:, ci]
                A = Pq[0][:, ci]
                atk, ak = "p3_PT0_%d" % ci, "p3_P0_%d" % ci
                P.tt("dve", tmpk[par][:], v4(ps[1][0:64, 0:256]), bc_last(nbv, 64), ALU.mult, ["ps1", gbk], [TK("tmpk")])
                P.tt("dve", qkdK[ks][:, ci], v4(ps[1][0:64, 256:512]), decm[par][:], ALU.mult, ["ps1", TK("decm")], [KK("qkd", ci)])
                P.tt("dve", AT, tmpk[par][:], decs[par][:], ALU.mult, [TK("tmpk"), TK("decs")], [atk])
                backs.append((ci, AT, A, atk, ak))
                if len(backs) >= 3:
                    emit_back(backs.pop(0))
                yield
            while backs:
                emit_back(backs.pop(0))
                yield
            for k in range(6):
                a_, b_ = k % 2, (k + 1) % 2
                for ci in range(CPS):
                    bA, bB = (3, 4) if ci % 2 == 0 else (6, 7)
                    bAk, bBk = "ps%d" % bA, "ps%d" % bB
                    Pc, PTc = Pq[a_][:, ci], PTq[a_][:, ci]
                    Pn, PTn = Pq[b_][:, ci], PTq[b_][:, ci]
                    pck, ptck = "p3_P%d_%d" % (a_, ci), "p3_PT%d_%d" % (a_, ci)
                    pnk, ptnk = "p3_P%d_%d" % (b_, ci), "p3_PT%d_%d" % (b_, ci)
                    Yc = YK[ks][:, ci]
                    if k <= 4:
                        for h in range(NH):
                            P.mm(ps[bA][0:64, h * 64:(h + 1) * 64], PTc[:, h, :], Pc[:, h, :], True, True, [pck, ptck], [bAk])
                    if k < 4:
                        for h in range(NH):
                            P.mm(ps[bB][0:64, 256 + h * 64:256 + (h + 1) * 64], Pc[:, h, :], PTc[:, h, :], True, True, [pck, ptck], [bBk])
                    if k >= 1:
                        for h in range(NH):
                            P.mm(ps[bB][0:64, h * 64:(h + 1) * 64], Pc[:, h, :], Yc[:, h, :], True, True, [pck, KK("Y", ci)], [bBk])
                    if k <= 4:
                        P.copy("act", Pn, v4(ps[bA][0:64, 0:256]), [bAk], [pnk])
                    if k < 4:
                        P.copy("act", PTn, v4(ps[bB][0:64, 256:512]), [bBk], [ptnk])
                    if k >= 1:
                        P.tt("dve", Yc, Yc, v4(ps[bB][0:64, 0:256]), ALU.add, [KK("Y", ci), bBk], [KK("Y", ci)])
                    yield
            for ci in range(CPS):
                smp = smK[ks][:, ci, :]
                smk = KK("sm", ci)
                egc, edec = smp[0:64, 0:4], smp[0:64, 16:20]
                kc = ktb[st_][:, ci, :].rearrange("p (h f) -> p h f", h=NH)
                P.tt("pool", Dg4[:, ci], bc_mid(id64, NH), bc_last(egc, 64), ALU.mult, ["consts", smk], ["p3_Dg4_%d" % ci])
                P.tt("pool", kdecK[ks][:, ci], kc, bc_last(edec, 128), ALU.mult, [ktk, smk], [KK("kdec", ci)])
            yield
            for ci in range(CPS):
                qTc = qTb[st_][:, :, ci * 64:(ci + 1) * 64]
                P.mm(ps[1][:, 0:256], ones64, Dg4[:, ci].rearrange("p h c -> p (h c)"), True, True, ["consts", "p3_Dg4_%d" % ci], ["ps1"])
                P.tt("dve", qgTK[ks][:, ci], qTc, v4(ps[1][:, 0:256]), ALU.mult, [qk_, "ps1"], [KK("qgT", ci)])
                yield

        def chain_gen(i):
            d, B = seq[i]
            st_ = i % NSET
            ks = i % 2
            kk, vtk, gbk = "p3_kTb%d" % st_, "p3_vtb%d" % st_, "p3_gbb%d" % st_
            KK = lambda n, ci: "p3k_%s_%d_%d" % (n, ks, ci)
            first_of_dir = (B == 0) if d == 0 else (B == NSEG - 1)
            if first_of_dir:
                P.dma("sp", S[:], g.s0[l, d].rearrange("h k v -> k h v"), writes=["p3_S"])
                P.copy("act", Sb[:], S[:], ["p3_S"], ["p3_Sb"])
            cis = list(range(CPS)) if d == 0 else list(range(CPS - 1, -1, -1))
            for n_, ci in enumerate(cis):
                c = B * CPS + ci
                par = (i * CPS + n_) % 2
                TK = lambda n: "p3_%s%d" % (n, par)
                smp = smK[ks][:, ci, :]
                smk = KK("sm", ci)
                egc, etot = smp[0:64, 0:4], smp[:, 4:8]
                nbv = gbb[st_][:, ci, 16 + d * 4:16 + d * 4 + 4]
                kTc = kTb[st_][:, :, ci * 64:(ci + 1) * 64]
                vc = vtb[st_][:, ci, :].rearrange("p (h f) -> p h f", h=NH)
                Yc, qkdc, qgTc, kdc = YK[ks][:, ci], qkdK[ks][:, ci], qgTK[ks][:, ci], kdecK[ks][:, ci]
                P.tt("pool", t2[:], S[:], bc_last(etot, 128), ALU.mult, ["p3_S", smk], ["p3_t2"])
                for h in range(NH):
                    P.mm(ps[5][0:64, h * 128:(h + 1) * 128], kTc[:, h, :], Sb[:, h, :], True, True, [kk, "p3_Sb"], ["ps5"])
                yield
                for h in range(NH):
                    P.stt("dve", nr[par][:, h, :], ps[5][0:64, h * 128:(h + 1) * 128], egc[:, h:h + 1], vc[:, h, :], ALU.mult, ALU.subtract,
                          ["ps5", smk, vtk], [TK("nr")])
                yield
                for h in range(NH):
                    P.mm(ps[5][0:64, h * 128:(h + 1) * 128], Yc[:, h, :], nr[par][:, h, :], True, True, [KK("Y", ci), TK("nr")], ["ps5"])
                yield
                P.tt("dve", vnew[par][:], v4(ps[5][0:64, :]), bc_last(nbv, 128), ALU.mult, ["ps5", gbk], [TK("vnew")])
                yield
                for h in range(NH):
                    o_ps = ps[2][:, 256 + h * 64:256 + (h + 1) * 64]
                    P.mm(o_ps, Sb[:, h, :], qgTc[:, h, :], True, False, ["p3_Sb", KK("qgT", ci)], ["ps2"])
                    P.mm(o_ps, vnew[par][:, h, :], qkdc[:, h, :], False, True, [TK("vnew"), KK("qkd", ci)], ["ps2"])
                for h in range(NH):
                    P.mm(ps[5][:, h * 128:(h + 1) * 128], kdc[:, h, :], vnew[par][:, h, :], True, True, [KK("kdec", ci), TK("vnew")], ["ps5"])
                o_sl = oT[:, :, c * 64:(c + 1) * 64]
                if d == 0:
                    P.copy("act", o_sl, v4(ps[2][:, 256:512]), ["ps2"], ["p3_oT"])
                else:
                    P.tt("dve", o_sl, o_sl, v4(ps[2][:, 256:512]), ALU.add, ["ps2", "p3_oT"], ["p3_oT"])
                P.tt("dve", S[:], t2[:], v4(ps[5][:, :]), ALU.add, ["p3_t2", "ps5"], ["p3_S"])
                if n_ == CPS - 1:
                    P.dma("pool", g.st[B, l, d].rearrange("h k v -> k h v"), S[:], reads=["p3_S"], writes=[])
                    last = (B == NSEG - 1) if d == 0 else (B == 0)
                    if not last:
                        P.ts("dve", S[:], S[:], g.keep[:, 0:1], None, ALU.mult, None, ["p3_S", "keep"], ["p3_S"])
                P.copy("act", Sb[:], S[:], ["p3_S"], ["p3_Sb"])
                yield

        load_block(0)
        for _ in pre_gen(0):
            pass
        for i in range(len(seq)):
            ch = chain_gen(i)
            pr = pre_gen(i + 1) if i + 1 < len(seq) else iter(())
            ch_done = pr_done = False
            while not (ch_done and pr_done):
                if not ch_done:
                    try:
                        next(ch)
                    except StopIteration:
                        ch_done = True
                for _ in range(4):
                    if not pr_done:
                        try:
                            next(pr)
                        except StopIteration:
                            pr_done = True
        P.barrier()
        pes2.close()
        sb = sb_outer
        zs = two("p3_zs", [128, NT], BF16)
        oo = two("p3_oo", [128, NT], F32)
        rs = two("p3_rs", [128, NT], F32)
        stg = two("p3_stg", [128, NT], BF16)
        for h in range(NH):
            i2 = h % 2
            z, zk = zs[i2], "p3_zs%d" % i2
            P.dma("sp", z[:], g.projT[1536 + h * 128:1536 + (h + 1) * 128, :], reads=["projT"], writes=[zk])
            P.actf(oo[i2][:], oT[:, h, :], AF.Square, ["p3_oT"], ["p3_oo%d" % i2])
            for T in range(NTT):
                tok = slice(T * 512, (T + 1) * 512)
                pb, pk = ps[T % 8], "ps%d" % (T % 8)
                P.mm(pb[:, :], g.ones, oo[i2][:, tok], True, True, ["consts", "p3_oo%d" % i2], [pk])
                P.actf(rs[i2][:, tok], pb[:, :], AF.Sqrt, [pk], ["p3_rs%d" % i2], bias=EPS, scale=1.0 / 128)
            P.op("dve", lambda e, r=rs[i2]: e.reciprocal(r[:], r[:]), ["p3_rs%d" % i2], ["p3_rs%d" % i2])
            P.tt("dve", rs[i2][:], oT[:, h, :], rs[i2][:], ALU.mult, ["p3_oT", "p3_rs%d" % i2], ["p3_rs%d" % i2])
            P.stt("dve", stg[i2][:], rs[i2][:], g.vp[:, OB_GON:OB_GON + 1], z[:], ALU.mult, ALU.mult,
                  ["p3_rs%d" % i2, "vp", zk], ["p3_stg%d" % i2])
            P.dma("pool", g.ogT[h * 128:(h + 1) * 128, :], stg[i2][:], reads=["p3_stg%d" % i2], writes=[])
        P.barrier()


def res_bufs(sb):
    return {
        "stat": [sb("rs_stat%d" % i, [128, 4], F32) for i in range(2)],
        "tmp": [sb("rs_tmp%d" % i, [128, 512], F32) for i in range(2)],
        "junk": sb("rs_junk", [128, 512], BF16),
    }


def residual_update(g, pbs, pks, x_ap, xkey, G, Gk, bufs, n):
    P = g.P
    stat, sk = bufs["stat"][n % 2], "rs_stat%d" % (n % 2)
    P.memset("dve", stat[:], 0.0, [sk])
    for hf in range(2):
        P.actf(bufs["junk"][:], pbs[hf][:, :], AF.Square, [pks[hf], sk], ["rs_junk", sk], accum=stat[:, hf:hf + 1])
    P.tt("dve", stat[:, 2:3], stat[:, 0:1], stat[:, 1:2], ALU.add, [sk], [sk])
    P.actf(stat[:, 3:4], stat[:, 2:3], AF.Sqrt, [sk], [sk], bias=EPS, scale=1.0 / D)
    P.op("dve", lambda e: e.reciprocal(stat[:, 3:4], stat[:, 3:4]), [sk], [sk])
    for hf in range(2):
        t, tk = bufs["tmp"][hf], "rs_tmp%d" % hf
        sl = slice(hf * 512, (hf + 1) * 512)
        P.stt("dve", t[:], pbs[hf][:, :], stat[:, 3:4], G[:, sl], ALU.mult, ALU.mult, [pks[hf], sk, Gk], [tk])
        P.tt("pool", x_ap[:, sl], x_ap[:, sl], t[:], ALU.add, [xkey, tk], [xkey])


def phase_p4a(g, l):
    P = g.P
    src = g.x_in if l == 0 else g.y
    srckey = "x_dram" if l == 0 else "y_dram"
    with ExitStack() as pes:
        sb = _sbf(g, pes)
        wbf = sb("p4a_w", [128, 8, D], BF16)
        WK = load_weight_bf16(g, sb, wbf, "p4a_w", g.w_out[l], 8, D, "p4aw")
        rb = res_bufs(sb)
        xt = [sb("p4a_x%d" % i, [128, 4, D], F32) for i in range(2)]
        mixT = [sb("p4a_m%d" % i, [128, 8, 512], BF16) for i in range(2)]
        n = 0
        for T in range(g.NTT):
            tok = slice(T * 512, (T + 1) * 512)
            x, xk = xt[T % 2], "p4a_x%d" % (T % 2)
            m, mk = mixT[T % 2], "p4a_m%d" % (T % 2)
            P.dma("sp", x[:], src[tok, :].rearrange("(s p) f -> p s f", p=128), reads=[], writes=[xk])
            P.dma("sp", m[:, 0:4, :], g.ogT.rearrange("(k p) t -> p k t", p=128)[:, :, tok], reads=["ogT"], writes=[mk])
            P.dma("sp", m[:, 4:8, :], g.uT.rearrange("(k p) t -> p k t", p=128)[:, :, tok], reads=["uT"], writes=[mk])
            for s in range(4):
                b0 = (n % 2) * 2
                pbs = [g.psum[b0], g.psum[b0 + 1]]
                pks = ["ps%d" % b0, "ps%d" % (b0 + 1)]
                for hf in range(2):
                    for kc in range(8):
                        P.mm(pbs[hf][:, :], m[:, kc, s * 128:(s + 1) * 128], wbf[:, kc, hf * 512:(hf + 1) * 512], kc == 0, kc == 7,
                             [mk, WK[kc]], [pks[hf]])
                residual_update(g, pbs, pks, x[:, s, :], xk, g.Gm, "Gm", rb, n)
                n += 1
            P.dma("pool", g.y[tok, :].rearrange("(s p) f -> p s f", p=128), x[:], reads=[xk], writes=[])
        P.barrier()


def phase_p4b(g, l):
    P = g.P
    NCC = 2 * DFF // 128
    with ExitStack() as pes:
        sb = _sbf(g, pes)
        wbf = sb("p4b_w", [128, 8, 2 * DFF], BF16)
        WK = load_weight_bf16(g, sb, wbf, "p4b_w", g.w_up[l], 8, 2 * DFF, "p4bw")
        nb = nh_bufs(sb)
        xt = [sb("p4b_x%d" % i, [128, 4, D], F32) for i in range(2)]
        hT = [sb("p4b_hT%d" % i, [128, 8, 512], BF16) for i in range(2)]
        stg = [sb("p4b_stg%d" % i, [128, 512], BF16) for i in range(4)]
        cnt = {"nsub": 0, "nst": 0, "npb": 0}

        def norm_tile(T):
            tok = slice(T * 512, (T + 1) * 512)
            x, xk = xt[T % 2], "p4b_x%d" % (T % 2)
            h, hk = hT[T % 2], "p4b_hT%d" % (T % 2)
            P.dma("sp", x[:], g.y[tok, :].rearrange("(s p) f -> p s f", p=128), reads=[], writes=[xk])
            for s in range(4):
                yield from norm_to_hT(g, x[:, s, :], [xk], 2, h[:, :, s * 128:(s + 1) * 128], hk, nb, cnt["nsub"])
                cnt["nsub"] += 1

        def mm_tile(T):
            tok = slice(T * 512, (T + 1) * 512)
            h, hk = hT[T % 2], "p4b_hT%d" % (T % 2)
            for cc in range(NCC):
                pb, pk = g.psum[cnt["npb"] % 4], "ps%d" % (cnt["npb"] % 4)
                cnt["npb"] += 1
                for kc in range(8):
                    P.mm(pb[:, :], wbf[:, kc, cc * 128:(cc + 1) * 128], h[:, kc, :], kc == 0, kc == 7, [WK[kc], hk], [pk])
                st_, stk = stg[cnt["nst"] % 4], "p4b_stg%d" % (cnt["nst"] % 4)
                cnt["nst"] += 1
                P.copy("act" if cc % 2 == 0 else "dve", st_[:], pb[:, :], [pk], [stk])
                if cc < 22:
                    P.dma("pool", g.gtT[cc * 128:(cc + 1) * 128, tok], st_[:], reads=[stk], writes=[])
                else:
                    P.dma("pool", g.upT[(cc - 22) * 128:(cc - 21) * 128, tok], st_[:], reads=[stk], writes=[])
                yield

        for _ in norm_tile(0):
            pass
        for T in range(g.NTT):
            interleave(mm_tile(T), norm_tile(T + 1) if T + 1 < g.NTT else None, 0.7)
        P.barrier()


def phase_p5(g, l):
    P = g.P
    NT, NSEG = g.NT, g.NSEG
    W = SEG + 2 * PADF
    with ExitStack() as pes:
        sb = _sbf(g, pes)
        cin = [sb("p5_cin%d" % i, [128, NSEG, W], BF16) for i in range(2)]
        xL = [sb("p5_xL%d" % i, [128, NSEG, W], BF16) for i in range(2)]
        xR = [sb("p5_xR%d" % i, [128, NSEG, W], BF16) for i in range(2)]
        upr = [sb("p5_up%d" % i, [128, NT], BF16) for i in range(2)]
        gate = [sb("p5_gate%d" % i, [128, NT], BF16) for i in range(2)]
        outb = [sb("p5_out%d" % i, [128, NT], BF16) for i in range(2)]
        dgf = [sb("p5_dg%d" % i, [128, 9, 128], BF16) for i in range(2)]
        for i in range(2):
            P.memset("dve", cin[i][:], 0.0, ["p5_cin%d" % i])
            P.memset("pool", xL[i][:], 0.0, ["p5_xL%d" % i])
            P.memset("pool", xR[i][:], 0.0, ["p5_xR%d" % i])
        npb = 0
        for c in range(22):
            i2 = c % 2
            ci, ck = cin[i2], "p5_cin%d" % i2
            xl, xlk = xL[i2], "p5_xL%d" % i2
            xr, xrk = xR[i2], "p5_xR%d" % i2
            src = g.gtT[c * 128:(c + 1) * 128, :].rearrange("p (s t) -> p s t", s=NSEG)
            P.dma("sp", ci[:, :, PADF:PADF + SEG], src, reads=["gtT"], writes=[ck])
            P.dma("sp", xl[:, :, PADF:PADF + SEG], src, reads=["gtT"], writes=[xlk])
            P.dma("sp", xr[:, :, PADF:PADF + SEG], src, reads=["gtT"], writes=[xrk])
            P.dma("sp", upr[i2][:], g.upT[c * 128:(c + 1) * 128, :], reads=["upT"], writes=["p5_up%d" % i2])
            eL = slice(PADF + 63, PADF + SEG, 64)
            eR = slice(PADF, PADF + SEG, 64)
            P.tt("dve", xl[:, :, eL], xl[:, :, eL], g.ML[:, :, eL], ALU.mult, [xlk, "Mpad0"], [xlk])
            P.tt("dve", xr[:, :, eR], xr[:, :, eR], g.MR[:, :, eR], ALU.mult, [xrk, "Mpad1"], [xrk])
            _halo(g, ci, ck, PADF)
            _halo(g, xl, xlk, PADF)
            _halo(g, xr, xrk, PADF)
            dg, dgk = dgf[i2], "p5_dg%d" % i2
            P.tt("dve", dg[:], bc_mid(g.identb[:], 9), bc_last(g.vp[:, OB_CFFN + c * 9:OB_CFFN + c * 9 + 9], 128), ALU.mult,
                 ["identb", "vp"], [dgk])
            gt_, gk = gate[i2], "p5_gate%d" % i2
            for T in range(g.NTT):
                pb, pk = g.psum[npb % 8], "ps%d" % (npb % 8)
                npb += 1
                o3 = pb[:, :].rearrange("p (s t) -> p s t", s=2)
                n = 0
                for dy in range(3):
                    for dx in range(3):
                        j = dy * 3 + dx
                        sh = PADF + (dy - 1) * 64 + (dx - 1)
                        srcb, sk = ((xl, xlk) if dx == 0 else ((ci, ck) if dx == 1 else (xr, xrk)))
                        P.mm(o3, dg[:, j, :], srcb[:, 2 * T:2 * T + 2, sh:sh + SEG], n == 0, n == 8, [dgk, sk], [pk])
                        n += 1
                P.actf(gt_[:, T * 512:(T + 1) * 512], pb[:, :], AF.Silu, [pk, "vp"], [gk], bias=g.vp[:, OB_BFF + c:OB_BFF + c + 1])
            P.tt("pool" if c % 2 == 0 else "dve", outb[i2][:], gt_[:], upr[i2][:], ALU.mult, [gk, "p5_up%d" % i2], ["p5_out%d" % i2])
            P.dma("pool", g.aT[c * 128:(c + 1) * 128, :], outb[i2][:], reads=["p5_out%d" % i2], writes=[])
        P.barrier()


def phase_p6(g, l):
    P = g.P
    KC = DFF // 128
    with ExitStack() as pes:
        sb = _sbf(g, pes)
        wbf = sb("p6_w", [128, KC, D], BF16)
        WK = load_weight_bf16(g, sb, wbf, "p6_w", g.w_down[l], KC, D, "p6w")
        rb = res_bufs(sb)
        xt = [sb("p6_x%d" % i, [128, 4, D], F32) for i in range(2)]
        aTb = [sb("p6_a%d" % i, [128, KC, 512], BF16) for i in range(2)]
        n = 0
        for T in range(g.NTT):
            tok = slice(T * 512, (T + 1) * 512)
            x, xk = xt[T % 2], "p6_x%d" % (T % 2)
            m, mk = aTb[T % 2], "p6_a%d" % (T % 2)
            P.dma("sp", x[:], g.y[tok, :].rearrange("(s p) f -> p s f", p=128), reads=[], writes=[xk])
            P.dma("sp", m[:], g.aT.rearrange("(k p) t -> p k t", p=128)[:, :, tok], reads=["aT"], writes=[mk])
            for s in range(4):
                b0 = (n % 2) * 2
                pbs = [g.psum[b0], g.psum[b0 + 1]]
                pks = ["ps%d" % b0, "ps%d" % (b0 + 1)]
                for hf in range(2):
                    for kc in range(KC):
                        P.mm(pbs[hf][:, :], m[:, kc, s * 128:(s + 1) * 128], wbf[:, kc, hf * 512:(hf + 1) * 512], kc == 0, kc == KC - 1,
                             [mk, WK[kc]], [pks[hf]])
                residual_update(g, pbs, pks, x[:, s, :], xk, g.Gf, "Gf", rb, n)
                n += 1
            P.dma("pool", g.y[tok, :].rearrange("(s p) f -> p s f", p=128), x[:], reads=[xk], writes=[])
        P.barrier()


NT_FULL = 4096
DEPTH_FULL = 4
_PROG = {}


def kernel(**inputs):
    inp = {k: np.asarray(v) for k, v in inputs.items()}
    NT, DEPTH = NT_FULL, DEPTH_FULL
    shared = make_shared(inp, DEPTH)
    xs = np.asarray(inp["x_sample"], np.float32)
    xp = np.asarray(inp["x_prompt"], np.float32)
    nb_s = xs.shape[0]
    per = xp.shape[0] // (8 - nb_s)
    maps = []
    for b in range(nb_s):
        maps.append(make_core_map(inp, shared, NT, DEPTH, True, xs[b], inp["c"][b], inp["state_delta"][b]))
    zeros_s = np.zeros((DEPTH, 2, NH, 128, 128), np.float32)
    for c in range(8 - nb_s):
        real = xp[c * per:(c + 1) * per].reshape(per * SEG, D)
        x_tok = np.concatenate([real] * (NT // (per * SEG)), axis=0)
        maps.append(make_core_map(inp, shared, NT, DEPTH, False, x_tok, inp["c_ctx"], zeros_s))
    if "nc" not in _PROG:
        _PROG["nc"] = build_program(NT, DEPTH)
    res = run_bass_kernel_spmd(_PROG["nc"], maps, core_ids=list(range(8)))
    outs = res.results
    y_sample = np.stack([np.asarray(outs[b]["y"], np.float32) for b in range(nb_s)])
    y_prompt = np.concatenate([np.asarray(outs[nb_s + c]["y"], np.float32)[:per * SEG].reshape(per, SEG, D)
                               for c in range(8 - nb_s)], axis=0)
    new_state = np.concatenate([np.asarray(outs[nb_s + c]["st"], np.float32)[:per] for c in range(8 - nb_s)], axis=0)
    return (y_prompt, y_sample, new_state)
```
